# Optimizing a Trainium2 kernel written in Bass

```python
import math
import jax, jax.numpy as jnp
from jax import lax
import numpy as np

D_MODEL = 1024
BATCH = 16
SEQ = 2048
DEPTH = 2

N_A = max(1, DEPTH // 2)
N_B = DEPTH - N_A
N_HEADS = 16
HEAD_DIM = D_MODEL // N_HEADS
D_FF = 2816
CONV_WIDTH = 3
Q_BLOCK = 128
RMS_EPS = 1e-6

kernel_name = "yoco_shortconv_fox_macaron_sandwich"


def rms_norm(x, g):
    xf = x.astype(jnp.float32)
    y = xf * lax.rsqrt(jnp.mean(xf * xf, axis=-1, keepdims=True) + RMS_EPS)
    return (y * g.astype(jnp.float32)).astype(x.dtype)


def swiglu(x, w_in, w_out):
    gate, up = jnp.split(x @ w_in, 2, axis=-1)
    return (jax.nn.silu(gate) * up) @ w_out


def short_conv_mixer(x, w_in, conv_k, w_out):
    b_gate, c_gate, h = jnp.split(x @ w_in, 3, axis=-1)
    u = c_gate * h
    y = lax.conv_general_dilated(
        u, conv_k[:, None, :].astype(u.dtype),
        window_strides=(1,), padding=[(CONV_WIDTH - 1, 0)],
        dimension_numbers=("NWC", "WIO", "NWC"),
        feature_group_count=D_MODEL)
    return (b_gate * y) @ w_out


def shared_kv(x, kv_g, kv_w, forget_b):
    bsz, seq, _ = x.shape
    p = rms_norm(x, kv_g) @ kv_w
    k = p[..., :D_MODEL].reshape(bsz, seq, N_HEADS, HEAD_DIM).transpose(0, 2, 1, 3)
    v = p[..., D_MODEL:2 * D_MODEL].reshape(bsz, seq, N_HEADS, HEAD_DIM).transpose(0, 2, 1, 3)
    f_logit = (p[..., 2 * D_MODEL:] + forget_b).astype(jnp.float32)
    log_f = jax.nn.log_sigmoid(f_logit)
    c = jnp.cumsum(log_f, axis=1).transpose(0, 2, 1)
    return k, v, c


def forgetting_attention(x, k, v, c, w_qg, w_o):
    bsz, seq, _ = x.shape
    n_blk = seq // Q_BLOCK
    q, gate = jnp.split(x @ w_qg, 2, axis=-1)
    q = q.reshape(bsz, seq, N_HEADS, HEAD_DIM).transpose(0, 2, 1, 3)
    q_blocks = q.reshape(bsz, N_HEADS, n_blk, Q_BLOCK, HEAD_DIM).transpose(2, 0, 1, 3, 4)
    c_blocks = c.reshape(bsz, N_HEADS, n_blk, Q_BLOCK).transpose(2, 0, 1, 3)
    k_pos = jnp.arange(seq)
    scale = 1.0 / math.sqrt(HEAD_DIM)

    def attend_block(args):
        qb, cb, i = args
        s = jnp.einsum("bhqd,bhkd->bhqk", qb, k, preferred_element_type=jnp.float32) * scale
        s = s + cb[..., None] - c[:, :, None, :]
        q_pos = i * Q_BLOCK + jnp.arange(Q_BLOCK)
        s = jnp.where(k_pos[None, :] <= q_pos[:, None], s, -jnp.inf)
        p = jax.nn.softmax(s, axis=-1)
        return jnp.einsum("bhqk,bhkd->bhqd", p.astype(v.dtype), v)

    o = lax.map(attend_block, (q_blocks, c_blocks, jnp.arange(n_blk)))
    o = o.transpose(1, 0, 3, 2, 4).reshape(bsz, seq, D_MODEL)
    return (jax.nn.sigmoid(gate) * o) @ w_o


def setup_inputs(seed: int = 0) -> dict:
    key = jax.random.key(seed)
    ks = jax.random.split(key, 24)
    f32 = jnp.float32

    def w(k, shape, fan_in):
        return jax.random.normal(k, shape, f32) * fan_in ** -0.5

    def gain(k, shape):
        return 1.0 + 0.05 * jax.random.normal(k, shape, f32)

    return {
        "x": jax.random.normal(ks[0], (BATCH, SEQ, D_MODEL), f32),
        "ffn1_pre_g": gain(ks[1], (DEPTH, D_MODEL)),
        "ffn1_post_g": gain(ks[2], (DEPTH, D_MODEL)),
        "ffn1_w_in": w(ks[3], (DEPTH, D_MODEL, 2 * D_FF), D_MODEL),
        "ffn1_w_out": w(ks[4], (DEPTH, D_FF, D_MODEL), D_FF),
        "mix_pre_g": gain(ks[5], (DEPTH, D_MODEL)),
        "mix_post_g": gain(ks[6], (DEPTH, D_MODEL)),
        "ffn2_pre_g": gain(ks[7], (DEPTH, D_MODEL)),
        "ffn2_post_g": gain(ks[8], (DEPTH, D_MODEL)),
        "ffn2_w_in": w(ks[9], (DEPTH, D_MODEL, 2 * D_FF), D_MODEL),
        "ffn2_w_out": w(ks[10], (DEPTH, D_FF, D_MODEL), D_FF),
        "conv_w_in": w(ks[11], (N_A, D_MODEL, 3 * D_MODEL), D_MODEL),
        "conv_k": w(ks[12], (N_A, CONV_WIDTH, D_MODEL), CONV_WIDTH),
        "conv_w_out": w(ks[13], (N_A, D_MODEL, D_MODEL), D_MODEL),
        "kv_g": gain(ks[14], (D_MODEL,)),
        "kv_w": w(ks[15], (D_MODEL, 2 * D_MODEL + N_HEADS), D_MODEL),
        "forget_b": jax.random.uniform(ks[16], (N_HEADS,), f32, 1.0, 3.0),
        "attn_w_qg": w(ks[17], (N_B, D_MODEL, 2 * D_MODEL), D_MODEL),
        "attn_w_o": w(ks[18], (N_B, D_MODEL, D_MODEL), D_MODEL),
    }


def reference(x, ffn1_pre_g, ffn1_post_g, ffn1_w_in, ffn1_w_out, mix_pre_g, mix_post_g,
              ffn2_pre_g, ffn2_post_g, ffn2_w_in, ffn2_w_out, conv_w_in, conv_k, conv_w_out,
              kv_g, kv_w, forget_b, attn_w_qg, attn_w_o):
    k = v = c = None
    for l in range(DEPTH):
        if l == N_A:
            k, v, c = shared_kv(x, kv_g, kv_w, forget_b)
        h = swiglu(rms_norm(x, ffn1_pre_g[l]), ffn1_w_in[l], ffn1_w_out[l])
        x = x + 0.5 * rms_norm(h, ffn1_post_g[l])
        xn = rms_norm(x, mix_pre_g[l])
        if l < N_A:
            m = short_conv_mixer(xn, conv_w_in[l], conv_k[l], conv_w_out[l])
        else:
            j = l - N_A
            m = forgetting_attention(xn, k, v, c, attn_w_qg[j], attn_w_o[j])
        x = x + rms_norm(m, mix_post_g[l])
        h = swiglu(rms_norm(x, ffn2_pre_g[l]), ffn2_w_in[l], ffn2_w_out[l])
        x = x + 0.5 * rms_norm(h, ffn2_post_g[l])
    return x
```

```python
import math
from contextlib import ExitStack

import numpy as np
import concourse.bass as bass
import concourse.mybir as mybir
from concourse.bass_utils import run_bass_kernel_spmd

F32 = mybir.dt.float32
BF16 = mybir.dt.bfloat16
ALU = mybir.AluOpType
AF = mybir.ActivationFunctionType

D = 1024
KC = 8
DFF = 2816
NJ = 22
NG = 11
T = 512
H = 16
DH = 64
EPS = 1e-6
NBIG = 4
NSMALL = 3
NSCR = 6
MASK_NEG = -30000.0
LN_HALF = math.log(0.5)


class Sched:
    ENG = ("pe", "act", "dve", "pool", "sp")

    def __init__(self, nc, stack):
        self.nc = nc
        self.stack = stack
        self.ops = {e: [] for e in self.ENG}
        self.cnt = {e: 0 for e in self.ENG}
        self.sem = {}
        self.dcnt = {}
        self.waited = {e: {} for e in self.ENG}
        self.lastw = {}
        self.readers = {}
        for e in ("pe", "act", "dve", "pool"):
            self.getsem(e)

    def getsem(self, name):
        if name not in self.sem:
            self.sem[name] = self.stack.enter_context(self.nc.semaphore("s_" + name))
        return self.sem[name]

    def _deps(self, eng, reads, writes):
        need = {}

        def add(tok, raw):
            if tok is None:
                return
            s, v = tok
            if s == eng and (eng == "pe" or not raw):
                return
            if v > need.get(s, 0):
                need[s] = v

        for g in reads:
            add(self.lastw.get(g), True)
        for g in writes:
            add(self.lastw.get(g), False)
            for s, v in self.readers.get(g, {}).items():
                add((s, v), False)
        out = []
        for s, v in need.items():
            if self.waited[eng].get(s, 0) < v:
                self.waited[eng][s] = v
                out.append((self.sem[s], v))
        return out

    def _commit(self, tok, reads, writes):
        for g in writes:
            self.lastw[g] = tok
            self.readers[g] = {}
        for g in reads:
            r = self.readers.setdefault(g, {})
            if tok[1] > r.get(tok[0], 0):
                r[tok[0]] = tok[1]

    def emit(self, eng, fn, reads=(), writes=()):
        waits = self._deps(eng, reads, writes)
        self.cnt[eng] += 1
        tok = (eng, self.cnt[eng])
        sem = self.sem[eng]

        def op(e, fn=fn, waits=waits, sem=sem):
            for s, v in waits:
                e.wait_ge(s, v)
            fn(e).then_inc(sem, 1)

        self.ops[eng].append(op)
        self._commit(tok, reads, writes)

    def dma(self, q, fn, reads, writes, semname):
        sem = self.getsem(semname)
        waits = [(s_, v_) for (s_, v_) in self._deps(q, reads, writes) if s_ is not sem]
        self.dcnt[semname] = self.dcnt.get(semname, 0) + 16
        tok = (semname, self.dcnt[semname])

        def op(e, fn=fn, waits=waits, sem=sem):
            for s, v in waits:
                e.wait_ge(s, v)
            fn(e).then_inc(sem, 16)

        self.ops[q].append(op)
        self._commit(tok, reads, writes)

    def final_wait(self, q, semnames):
        items = [(self.sem[n], self.dcnt[n]) for n in semnames if n in self.dcnt]

        def op(e, items=items):
            for s, v in items:
                e.wait_ge(s, v)

        self.ops[q].append(op)


def build_program(n_seq=2, seq=2048, stop_stage=7):
    nc = bass.Bass("TRN2", target_bir_lowering=False)
    ntok = n_seq * seq
    tps = seq // T
    ntiles_total = n_seq * tps
    NQ = seq // 128

    def din(name, shape):
        return nc.dram_tensor(name, list(shape), F32, kind="ExternalInput").ap()

    x_d = din("x", [ntok, D])
    y_d = nc.dram_tensor("y", [ntok, D], F32, kind="ExternalOutput").ap()
    G_d = din("gains", [13, D])
    fb_d = din("fb", [1, H])
    kcv_d = din("kcv", [128, 24])
    f1wi = din("f1wi", [2, D, 2 * DFF])
    f1wo = din("f1wo", [2, DFF, D])
    f2wi = din("f2wi", [2, D, 2 * DFF])
    f2wo = din("f2wo", [2, DFF, D])
    cwi = din("cwi", [D, 3 * D])
    cwo = din("cwo", [D, D])
    kvw = din("kvw", [D, 2 * D + H])
    wqg = din("wqg", [D, 2 * D])
    wo_d = din("wo", [D, D])

    with ExitStack() as st:
        def sb(name, shape, dt):
            return st.enter_context(nc.sbuf_tensor(name, list(shape), dt))

        Xb = [sb(f"Xb{i}", [128, 4, D], F32) for i in range(2)]
        KT = sb("KT", [128, KC, seq], BF16)
        VA = sb("VA", [128, NQ, H, DH + 1], BF16)
        spt = sb("spt", [128, NQ, H], F32)
        negc = sb("negc", [128, NQ, H], F32)
        carry = sb("carry", [128, H], F32)
        xn = [sb(f"xn{i}", [128, D], BF16) for i in range(4)]
        xnT = sb("xnT", [128, KC, T], BF16)
        work = sb("work", [128, 22 * 512], BF16)
        big = [sb(f"big{i}", [128, 4096], BF16) for i in range(NBIG)]
        small = [sb(f"small{i}", [128, 1024], BF16) for i in range(NSMALL)]
        gpre = sb("gpre", [128, D], F32)
        gpost = sb("gpost", [128, D], F32)
        scr = [sb(f"scr{i}", [128, 520], F32) for i in range(NSCR)]
        junk = sb("junk", [128, D], BF16)
        stt = sb("stt", [128, 80], F32)
        SELc = sb("SELc", [128, H], BF16)
        QTz = [sb(f"QTz{i}", [128, T], BF16) for i in range(4)]
        PC = sb("PC", [128, 4, 2, 4, H], BF16)
        R1 = sb("R1", [128, 4, H], F32)
        CQT = sb("CQT", [128, T], BF16)
        RD = sb("RD", [128, 2 * T], BF16)
        SELB = sb("SELB", [128, 64], BF16)
        halo = sb("halo", [128, KC, 2], F32)
        kcv = sb("kcv_sb", [128, 24], F32)
        fb = sb("fb_sb", [128, H], F32)
        zf = sb("zf", [128, H], F32)
        ident = sb("ident", [128, 128], BF16)
        maskM = sb("maskM", [128, 128], BF16)
        Umat = sb("Umat", [128, 128], F32)
        ONES = sb("ONES", [128, 128], F32)
        ps = [st.enter_context(nc.psum_tensor(f"ps{i}", [128, 512], F32)) for i in range(8)]
        psb = [p.bitcast(BF16) for p in ps]

        S = Sched(nc, st)
        state = {"big": 0, "small": 0, "st": 0, "bank": 0}

        def wblk(b0, n=1):
            return [("work", b) for b in range(b0, b0 + n)]

        def wv(b0, n=1):
            return work[:, b0 * 512:(b0 + n) * 512]

        def stcols(n):
            c = state["st"]
            if c + n > 80:
                c = 0
            state["st"] = c + n
            return c

        S.emit("pool", lambda e: e.memset(ident[:], 0.0), writes=["ident"])
        S.emit("pool", lambda e: e.affine_select(out=ident[:], in_=ident[:], pattern=[[-1, 128]],
                                                 compare_op=ALU.not_equal, fill=1.0, base=0,
                                                 channel_multiplier=1),
               reads=["ident"], writes=["ident"])
        S.emit("pool", lambda e: e.memset(maskM[:], 0.0), writes=["maskM"])
        S.emit("pool", lambda e: e.affine_select(out=maskM[:], in_=maskM[:], pattern=[[1, 128]],
                                                 compare_op=ALU.is_ge, fill=MASK_NEG, base=0,
                                                 channel_multiplier=-1),
               reads=["maskM"], writes=["maskM"])
        S.emit("pool", lambda e: e.memset(Umat[:], 1.0), writes=["Umat"])
        S.emit("pool", lambda e: e.affine_select(out=Umat[:], in_=Umat[:], pattern=[[1, 128]],
                                                 compare_op=ALU.is_ge, fill=0.0, base=0,
                                                 channel_multiplier=-1),
               reads=["Umat"], writes=["Umat"])
        S.emit("pool", lambda e: e.memset(ONES[:], 1.0), writes=["ONES"])
        S.emit("pool", lambda e: e.memset(SELc[:], 0.0), writes=["SEL"])
        for off in (0, 16, 32):
            S.emit("pool", lambda e, off=off: e.affine_select(out=SELc[:], in_=SELc[:], pattern=[[-1, H]],
                                                             compare_op=ALU.not_equal, fill=1.0, base=-off,
                                                             channel_multiplier=1),
                   reads=["SEL"], writes=["SEL"])
        for i in range(4):
            S.emit("pool", lambda e, i=i: e.memset(QTz[i][:], 0.0), writes=[("QTz", i)])
        S.emit("pool", lambda e: e.memset(PC[:], 0.0), writes=["PC"])
        S.emit("pool", lambda e: e.memset(RD[:], 0.0), writes=["RD"])
        S.emit("pool", lambda e: e.memset(SELB[:], 0.0), writes=["SELB"])
        S.emit("pool", lambda e: e.memset(SELB[64:65, :], 1.0), writes=["SELB"])
        S.emit("pool", lambda e: e.memset(SELB[96:97, :], 1.0), writes=["SELB"])
        S.emit("pool", lambda e: e.memset(VA[:, :, :, DH:DH + 1], 1.0),
               writes=[("VA", q, hf) for q in range(NQ) for hf in range(2)])
        S.dma("sp", lambda e: e.dma_start(out=kcv[:], in_=kcv_d), [], ["kcv"], "cst0")
        S.dma("sp", lambda e: e.dma_start(out=fb[:], in_=fb_d.broadcast_to([128, H])), [], ["fb"], "cst1")

        units = {}
        ulist = []

        def mk(key, n, parts, ring):
            units[key] = dict(idx=len(ulist), n=n, parts=parts, ring=ring)
            ulist.append(key)

        def kcp(ap):
            return ap.rearrange("(kc p) n -> p kc n", p=128)

        def cpn(ap):
            return ap.rearrange("(c p) n -> p c n", p=128)

        def v3(ap, a):
            return ap.rearrange("p (a b) -> p a b", a=a)

        def sl3(a, b, k):
            return lambda s: v3(s[:, a:b], k)

        ffn_w = [(f1wi[0], f1wo[0]), (f2wi[0], f2wo[0]), (f1wi[1], f1wo[1]), (f2wi[1], f2wo[1])]

        def mk_ffn(f):
            wi, wo = ffn_w[f]
            for g in range(NG):
                mk(("wi", f, g), 4096, [(sl3(0, 2048, KC), kcp(wi[:, g * 256:(g + 1) * 256])),
                                        (sl3(2048, 4096, KC), kcp(wi[:, DFF + g * 256:DFF + (g + 1) * 256]))], "big")
                mk(("woA", f, g), 1024, [(sl3(0, 1024, 2), cpn(wo[g * 256:(g + 1) * 256, 0:512]))], "small")
            for gb in range(3):
                nch = min(8, NJ - 8 * gb)
                mk(("woB", f, gb), nch * 512, [(sl3(0, nch * 512, nch), cpn(wo[gb * 1024:gb * 1024 + nch * 128, 512:1024]))], "big")

        mk_ffn(0)
        for m in range(KC):
            mk(("cwi", m), 3072, [(sl3(i * 1024, (i + 1) * 1024, KC), kcp(cwi[:, i * D + m * 128:i * D + (m + 1) * 128]))
                                  for i in range(3)], "big")
        for hh in range(2):
            mk(("cwo", hh), 4096, [(sl3(0, 4096, 4), cpn(cwo[hh * 512:(hh + 1) * 512, :]))], "big")
        mk_ffn(1)
        for mg in range(2):
            mk(("kvk", mg), 4096, [(sl3(0, 4096, KC), kcp(kvw[:, mg * 512:(mg + 1) * 512]))], "big")
        for hf in range(2):
            mk(("kvv", hf), 4096, [(sl3(0, 4096, KC), kcp(kvw[:, D + hf * 512:D + (hf + 1) * 512]))], "big")
        mk(("kvf",), 128, [(sl3(0, 128, KC), kcp(kvw[:, 2 * D:2 * D + H]))], "small")
        mk_ffn(2)
        for mg in range(2):
            mk(("wq", mg), 4096, [(sl3(0, 4096, KC), kcp(wqg[:, mg * 512:(mg + 1) * 512]))], "big")
        for hf in range(2):
            mk(("wg", hf), 4096, [(sl3(0, 4096, KC), kcp(wqg[:, D + hf * 512:D + (hf + 1) * 512]))], "big")
        for hh in range(2):
            mk(("wo", hh), 4096, [(sl3(0, 4096, 4), cpn(wo_d[hh * 512:(hh + 1) * 512, :]))], "big")
        mk_ffn(3)

        use_scratch = ntiles_total > 1
        if use_scratch:
            sc = nc.dram_tensor("wscratch", [len(ulist), 128, 4096], BF16).ap()
        cv = {"list": [], "pos": 0}
        written_back = set()

        def emit_conv(n):
            if not use_scratch:
                return
            while n > 0 and cv["pos"] < len(cv["list"]):
                key, i = cv["list"][cv["pos"]]
                cv["pos"] += 1
                n -= 1
                u = units[key]
                dfn, src = u["parts"][i]
                dst = dfn(sc[u["idx"]])
                S.dma("pool", lambda e, dst=dst, src=src: e.dma_start(out=dst, in_=src), [], ["cvall"], "cv")

        def load_gain(dst, gran, idx, semname):
            S.dma("sp", lambda e: e.dma_start(out=dst[:], in_=G_d[idx:idx + 1, :].broadcast_to([128, D])),
                  [], [gran], semname)

        def load_unit(key, from_scratch):
            u = units[key]
            if u["ring"] == "big":
                k = state["big"] % NBIG
                state["big"] += 1
                slot, gran, semname = big[k], ("big", k), f"big{k}"
            else:
                k = state["small"] % NSMALL
                state["small"] += 1
                slot, gran, semname = small[k], ("small", k), f"small{k}"
            if from_scratch:
                n = u["n"]
                src = sc[u["idx"], :, 0:n]
                S.dma("pool", lambda e, slot=slot, src=src, n=n: e.dma_start(out=slot[:, 0:n], in_=src),
                      [("sc", key)], [gran], semname)
            else:
                for dfn, src in u["parts"]:
                    S.dma("pool", lambda e, dfn=dfn, src=src, slot=slot: e.dma_start(out=dfn(slot), in_=src),
                          [], [gran], semname)
                if use_scratch and key not in written_back:
                    written_back.add(key)
                    n = u["n"]
                    dst = sc[u["idx"], :, 0:n]
                    S.dma("sp", lambda e, slot=slot, dst=dst, n=n: e.dma_start(out=dst, in_=slot[:, 0:n]),
                          [gran], [("sc", key)], "wb_" + semname)
            return slot, [gran]

        nst = {}

        def norm_A1(Xc, xp, t):
            c = stcols(3)
            nst[t] = c
            S.emit("act", lambda e: e.activation(out=junk[:], in_=Xc[:, t, :], func=AF.Square,
                                                 scale=1.0 / 32.0, accum_out=stt[:, c:c + 1]),
                   reads=[("X", xp, t)], writes=["junk", ("st", c)])
            S.emit("act", lambda e: e.activation(out=stt[:, c + 1:c + 2], in_=stt[:, c:c + 1], func=AF.Ln, bias=EPS),
                   reads=[("st", c)], writes=[("st", c + 1)])
            S.emit("act", lambda e: e.activation(out=stt[:, c + 2:c + 3], in_=stt[:, c + 1:c + 2], func=AF.Exp, scale=-0.5),
                   reads=[("st", c + 1)], writes=[("st", c + 2)])

        def norm_A2(Xc, xp, t):
            c = nst.pop(t)
            S.emit("dve", lambda e: e.scalar_tensor_tensor(out=xn[t][:], in0=Xc[:, t, :], scalar=stt[:, c + 2:c + 3],
                                                           in1=gpre[:], op0=ALU.mult, op1=ALU.mult),
                   reads=[("X", xp, t), ("st", c + 2), "gpre"], writes=[("xn", t)])

        def norm_A(Xc, xp, t):
            norm_A1(Xc, xp, t)
            norm_A2(Xc, xp, t)

        def norm_B(t, split=False):
            def tr(e):
                last = None
                for m in range(KC):
                    last = e.transpose(out=psb[t][:, m * 128:(m + 1) * 128],
                                       in_=xn[t][:, m * 128:(m + 1) * 128], identity=ident[:])
                return last
            S.emit("pe", tr, reads=[("xn", t), "ident"], writes=[("ps", t)])
            if split:
                S.emit("act", lambda e: e.activation(out=xnT[:, 0:4, t * 128:(t + 1) * 128],
                                                     in_=v3(psb[t][:, 0:512], 4), func=AF.Copy),
                       reads=[("ps", t)], writes=[("xnT", t)])
                S.emit("dve", lambda e: e.tensor_copy(out=xnT[:, 4:8, t * 128:(t + 1) * 128],
                                                      in_=v3(psb[t][:, 512:1024], 4)),
                       reads=[("ps", t)], writes=[("xnT", t)])
            else:
                S.emit("act", lambda e: e.activation(out=xnT[:, :, t * 128:(t + 1) * 128],
                                                     in_=v3(psb[t][:, :], KC), func=AF.Copy),
                       reads=[("ps", t)], writes=[("xnT", t)])

        XNT = [("xnT", t) for t in range(4)]

        sqA = {}

        def epi_sqA(t):
            c = stcols(5)
            sqA[t] = c
            S.emit("act", lambda e: e.activation(out=junk[:, 0:512], in_=ps[4 + t][:], func=AF.Square,
                                                 scale=1.0 / 32.0, accum_out=stt[:, c:c + 1]),
                   reads=[("ps", 4 + t)], writes=["junk", ("st", c)])

        def epilogue_t(Xc, xp, half, t, hook1=None, hook2=None):
            if t not in sqA:
                epi_sqA(t)
            c = sqA.pop(t)
            S.emit("act", lambda e: e.activation(out=junk[:, 512:1024], in_=ps[t][:], func=AF.Square,
                                                 scale=1.0 / 32.0, accum_out=stt[:, c + 1:c + 2]),
                   reads=[("ps", t)], writes=["junk", ("st", c + 1)])
            if hook1 is not None:
                hook1()
            S.emit("dve", lambda e: e.tensor_tensor(out=stt[:, c + 2:c + 3], in0=stt[:, c:c + 1],
                                                    in1=stt[:, c + 1:c + 2], op=ALU.add),
                   reads=[("st", c), ("st", c + 1)], writes=[("st", c + 2)])

            S.emit("act", lambda e: e.activation(out=stt[:, c + 3:c + 4], in_=stt[:, c + 2:c + 3], func=AF.Ln, bias=EPS),
                   reads=[("st", c + 2)], writes=[("st", c + 3)])
            S.emit("act", lambda e: e.activation(out=stt[:, c + 4:c + 5], in_=stt[:, c + 3:c + 4], func=AF.Exp,
                                                 scale=-0.5, bias=(LN_HALF if half else 0.0)),
                   reads=[("st", c + 3)], writes=[("st", c + 4)])
            for hf, bank in enumerate((4 + t, t)):
                S.emit("dve", lambda e, hf=hf, bank=bank: e.scalar_tensor_tensor(
                    out=scr[4 + hf][:, 0:512], in0=ps[bank][:], scalar=stt[:, c + 4:c + 5],
                    in1=gpost[:, hf * 512:(hf + 1) * 512], op0=ALU.mult, op1=ALU.mult),
                    reads=[("ps", bank), ("st", c + 4), "gpost"], writes=[("scr", 4 + hf)])
            for hf in range(2):
                S.emit("dve", lambda e, hf=hf: e.tensor_tensor(
                    out=Xc[:, t, hf * 512:(hf + 1) * 512], in0=scr[4 + hf][:, 0:512],
                    in1=Xc[:, t, hf * 512:(hf + 1) * 512], op=ALU.add),
                    reads=[("scr", 4 + hf), ("X", xp, t)], writes=[("X", xp, t)])
            if hook2 is not None:
                hook2()

        def sigmoid_chain(src_bank_ap, src_gran, dst_ap, dst_gran, si):
            tmp = scr[si][:, 0:512]
            S.emit("act", lambda e: e.activation(out=tmp, in_=src_bank_ap, func=AF.Exp, scale=-1.0),
                   reads=[src_gran], writes=[("scr", si)])
            S.emit("act", lambda e: e.activation(out=tmp, in_=tmp, func=AF.Ln, bias=1.0),
                   reads=[("scr", si)], writes=[("scr", si)])
            S.emit("act", lambda e: e.activation(out=dst_ap, in_=tmp, func=AF.Exp, scale=-1.0),
                   reads=[("scr", si)], writes=dst_gran)

        def ffn_body(f, fs, after_t, chunk_hook=None, mid_hook=None):
            pend = None

            def emit_DA(j, sslot, sgr, jj):
                def fn(e):
                    last = None
                    for t in range(4):
                        last = e.matmul(ps[4 + t][:], lhsT=wv(j)[:, t * 128:(t + 1) * 128],
                                        rhs=sslot[:, jj * 512:(jj + 1) * 512],
                                        start=(j == 0), stop=(j == NJ - 1))
                    return last
                S.emit("pe", fn, reads=wblk(j) + sgr, writes=[("ps", 4 + t) for t in range(4)])

            for g in range(NG):
                slot, bgr = load_unit(("wi", f, g), fs)
                sslot, sgr = load_unit(("woA", f, g), fs)
                for jj in range(2):
                    j = 2 * g + jj
                    pg, pu = (0, 1) if j % 2 == 0 else (2, 3)
                    for which, bank in ((0, pg), (1, pu)):
                        def fn(e, which=which, bank=bank, jj=jj, slot=slot):
                            last = None
                            for kc in range(KC):
                                base = which * 2048 + kc * 256 + jj * 128
                                last = e.matmul(ps[bank][:], lhsT=slot[:, base:base + 128], rhs=xnT[:, kc, :],
                                                start=(kc == 0), stop=(kc == KC - 1))
                            return last
                        S.emit("pe", fn, reads=bgr + XNT, writes=[("ps", bank)])
                    if pend is not None:
                        emit_DA(*pend)
                    si = j % 2
                    sigmoid_chain(ps[pg][:], ("ps", pg), scr[si][:, 0:512], [("scr", si)], si)
                    S.emit("dve", lambda e, si=si, pg=pg: e.tensor_tensor(out=scr[2 + si][:, 0:512], in0=ps[pg][:],
                                                                          in1=scr[si][:, 0:512], op=ALU.mult),
                           reads=[("ps", pg), ("scr", si)], writes=[("scr", 2 + si)])
                    S.emit("dve", lambda e, si=si, pu=pu, j=j: e.tensor_tensor(out=wv(j), in0=ps[pu][:],
                                                                              in1=scr[2 + si][:, 0:512], op=ALU.mult),
                           reads=[("ps", pu), ("scr", 2 + si)], writes=wblk(j))
                    pend = (j, sslot, sgr, jj)
                    if chunk_hook is not None:
                        chunk_hook(j)
            bslots = None

            def emit_B(t, gb):
                nch = min(8, NJ - 8 * gb)
                slot, bgr = bslots[gb]

                def fn(e, t=t, gb=gb, nch=nch, slot=slot):
                    last = None
                    for c_ in range(nch):
                        j = 8 * gb + c_
                        last = e.matmul(ps[t][:], lhsT=wv(j)[:, t * 128:(t + 1) * 128],
                                        rhs=slot[:, c_ * 512:(c_ + 1) * 512],
                                        start=(j == 0), stop=(j == NJ - 1))
                    return last
                S.emit("pe", fn, reads=wblk(8 * gb, nch) + bgr, writes=[("ps", t)])

            early = []
            if mid_hook is None:
                bslots = [load_unit(("woB", f, gb), fs) for gb in range(3)]
                for gb in range(2):
                    emit_B(0, gb)
                    early.append((0, gb))
            emit_DA(*pend)
            if mid_hook is not None:
                mid_hook()
            for t in range(4):
                epi_sqA(t)
            if bslots is None:
                bslots = [load_unit(("woB", f, gb), fs) for gb in range(3)]
            for t in range(4):
                for gb in range(3):
                    if (t, gb) not in early:
                        emit_B(t, gb)
                after_t(t)

        def down_proj(lhs_fn, lhs_reads, key, fs, after_t):
            slots = [load_unit((key, hh), fs) for hh in range(2)]
            for t in range(4):
                for hf, bank in enumerate((4 + t, t)):
                    def fn(e, t=t, hf=hf, bank=bank):
                        last = None
                        for m in range(KC):
                            slot = slots[m // 4][0]
                            base = (m % 4) * 1024 + hf * 512
                            last = e.matmul(ps[bank][:], lhsT=lhs_fn(m, t), rhs=slot[:, base:base + 512],
                                            start=(m == 0), stop=(m == KC - 1))
                        return last
                    S.emit("pe", fn, reads=lhs_reads(t) + slots[0][1] + slots[1][1], writes=[("ps", bank)])
                after_t(t)

        def conv_body(first_tile, fs, after_t):
            if first_tile:
                S.emit("dve", lambda e: e.memset(halo[:], 0.0), writes=[("halo", m) for m in range(KC)])
            for m in range(KC):
                slot, bgr = load_unit(("cwi", m), fs)
                banks = (0, 1, 2) if m % 2 == 0 else (3, 4, 5)
                for i in range(3):
                    def fn(e, i=i, slot=slot, bank=banks[i]):
                        last = None
                        for kc in range(KC):
                            base = i * 1024 + kc * 128
                            last = e.matmul(ps[bank][:], lhsT=slot[:, base:base + 128], rhs=xnT[:, kc, :],
                                            start=(kc == 0), stop=(kc == KC - 1))
                        return last
                    S.emit("pe", fn, reads=bgr + XNT, writes=[("ps", banks[i])])
                pb_, pc_, ph_ = banks
                ci = m % 2
                ui = 2 + m % 2
                yi = 4 + m % 2
                S.emit("act", lambda e, ci=ci, pc_=pc_: e.activation(out=scr[ci][:, 0:512], in_=ps[pc_][:], func=AF.Copy),
                       reads=[("ps", pc_)], writes=[("scr", ci)])
                S.emit("act", lambda e, ui=ui, m=m: e.activation(out=scr[ui][:, 0:2], in_=halo[:, m, :], func=AF.Copy),
                       reads=[("halo", m)], writes=[("scr", ui)])
                S.emit("dve", lambda e, ui=ui, ci=ci, ph_=ph_: e.tensor_tensor(out=scr[ui][:, 2:514], in0=ps[ph_][:],
                                                                                in1=scr[ci][:, 0:512], op=ALU.mult),
                       reads=[("ps", ph_), ("scr", ci), ("scr", ui)], writes=[("scr", ui)])
                S.emit("act", lambda e, ui=ui, m=m: e.activation(out=halo[:, m, :], in_=scr[ui][:, 512:514], func=AF.Copy),
                       reads=[("scr", ui)], writes=[("halo", m)])
                S.emit("dve", lambda e, ui=ui, yi=yi, m=m: e.tensor_scalar(
                    out=scr[yi][:, 0:512], in0=scr[ui][:, 2:514], scalar1=kcv[:, m * 3 + 2:m * 3 + 3],
                    scalar2=None, op0=ALU.mult),
                    reads=[("scr", ui), "kcv"], writes=[("scr", yi)])
                for w_, off in ((1, 1), (0, 0)):
                    S.emit("dve", lambda e, ui=ui, yi=yi, m=m, w_=w_, off=off: e.scalar_tensor_tensor(
                        out=scr[yi][:, 0:512], in0=scr[ui][:, off:off + 512], scalar=kcv[:, m * 3 + w_:m * 3 + w_ + 1],
                        in1=scr[yi][:, 0:512], op0=ALU.mult, op1=ALU.add),
                        reads=[("scr", ui), ("scr", yi), "kcv"], writes=[("scr", yi)])
                S.emit("dve", lambda e, yi=yi, m=m, pb_=pb_: e.tensor_tensor(out=wv(m), in0=ps[pb_][:],
                                                                             in1=scr[yi][:, 0:512], op=ALU.mult),
                       reads=[("ps", pb_), ("scr", yi)], writes=wblk(m))
            down_proj(lambda m, t: wv(m)[:, t * 128:(t + 1) * 128], lambda t: wblk(0, 8), "cwo", fs, after_t)

        def kv_body(tile_in_seq, fs):
            tq0 = tile_in_seq * 4
            if tile_in_seq == 0:
                S.emit("dve", lambda e: e.memset(carry[:], 0.0), writes=["carry"])
            fslot, fgr = load_unit(("kvf",), fs)
            for t in range(4):
                bank = 4 + t

                def fn(e, t=t, bank=bank):
                    last = None
                    for kc in range(KC):
                        last = e.matmul(ps[bank][:, 0:H], lhsT=xnT[:, kc, t * 128:(t + 1) * 128],
                                        rhs=fslot[:, kc * H:(kc + 1) * H],
                                        start=(kc == 0), stop=(kc == KC - 1))
                    return last
                S.emit("pe", fn, reads=fgr + [("xnT", t)], writes=[("ps", bank)])
                S.emit("dve", lambda e, bank=bank: e.tensor_tensor(out=zf[:], in0=ps[bank][:, 0:H], in1=fb[:], op=ALU.add),
                       reads=[("ps", bank), "fb"], writes=["zf"])
                S.emit("act", lambda e: e.activation(out=zf[:], in_=zf[:], func=AF.Exp, scale=-1.0),
                       reads=["zf"], writes=["zf"])
                S.emit("act", lambda e, t=t: e.activation(out=spt[:, tq0 + t, :], in_=zf[:], func=AF.Ln, bias=1.0),
                       reads=["zf"], writes=[("spt", tq0 + t)])
            for mg in range(2):
                slot, bgr = load_unit(("kvk", mg), fs)
                for mm in range(4):
                    m = mg * 4 + mm
                    bank = m % 4

                    def fn(e, mm=mm, slot=slot, bank=bank):
                        last = None
                        for kc in range(KC):
                            base = kc * 512 + mm * 128
                            last = e.matmul(ps[bank][:], lhsT=slot[:, base:base + 128], rhs=xnT[:, kc, :],
                                            start=(kc == 0), stop=(kc == KC - 1))
                        return last
                    S.emit("pe", fn, reads=bgr + XNT, writes=[("ps", bank)])
                    if m % 2 == 0:
                        S.emit("act", lambda e, m=m, bank=bank: e.activation(
                            out=KT[:, m, tile_in_seq * T:(tile_in_seq + 1) * T], in_=ps[bank][:], func=AF.Copy),
                            reads=[("ps", bank)], writes=[("KT", tile_in_seq, m)])
                    else:
                        S.emit("dve", lambda e, m=m, bank=bank: e.tensor_copy(
                            out=KT[:, m, tile_in_seq * T:(tile_in_seq + 1) * T], in_=ps[bank][:]),
                            reads=[("ps", bank)], writes=[("KT", tile_in_seq, m)])
            sgr = [("spt", tq0 + t) for t in range(4)]
            S.emit("pe", lambda e: e.matmul(ps[4][:, 0:4 * H], lhsT=Umat[:], rhs=spt[:, tq0:tq0 + 4, :],
                                            start=True, stop=True),
                   reads=sgr + ["Umat"], writes=[("ps", 4)])
            S.emit("pe", lambda e: e.matmul(ps[5][:, 0:4 * H], lhsT=ONES[:], rhs=spt[:, tq0:tq0 + 4, :],
                                            start=True, stop=True),
                   reads=sgr + ["ONES"], writes=[("ps", 5)])
            for t in range(4):
                S.emit("dve", lambda e, t=t: e.tensor_tensor(out=negc[:, tq0 + t, :], in0=ps[4][:, t * H:(t + 1) * H],
                                                             in1=carry[:], op=ALU.add),
                       reads=[("ps", 4), "carry"], writes=[("negc", tq0 + t)])
                S.emit("dve", lambda e, t=t: e.tensor_tensor(out=carry[:], in0=ps[5][:, t * H:(t + 1) * H],
                                                             in1=carry[:], op=ALU.add),
                       reads=[("ps", 5), "carry"], writes=["carry"])
            vslots = [load_unit(("kvv", hf), fs) for hf in range(2)]
            for t in range(4):
                for hf in range(2):
                    bank = 4 + (t * 2 + hf) % 4
                    slot, bgr = vslots[hf]

                    def fn(e, t=t, slot=slot, bank=bank):
                        last = None
                        for kc in range(KC):
                            last = e.matmul(ps[bank][:], lhsT=xnT[:, kc, t * 128:(t + 1) * 128],
                                            rhs=slot[:, kc * 512:(kc + 1) * 512],
                                            start=(kc == 0), stop=(kc == KC - 1))
                        return last
                    S.emit("pe", fn, reads=bgr + [("xnT", t)], writes=[("ps", bank)])
                    dst = VA[:, tq0 + t, hf * 8:(hf + 1) * 8, 0:DH]
                    if (t + hf) % 2 == 0:
                        S.emit("act", lambda e, dst=dst, bank=bank: e.activation(out=dst, in_=v3(ps[bank][:], 8), func=AF.Copy),
                               reads=[("ps", bank)], writes=[("VA", tq0 + t, hf)])
                    else:
                        S.emit("dve", lambda e, dst=dst, bank=bank: e.tensor_copy(out=dst, in_=v3(ps[bank][:], 8)),
                               reads=[("ps", bank)], writes=[("VA", tq0 + t, hf)])

        def attn_body(tile_in_seq, fs, after_t):
            tq0 = tile_in_seq * 4
            nj = tq0 + 4
            nsl = negc[:, tq0:tq0 + 4, :]
            ngr = [("negc", tq0 + r) for r in range(4)]
            S.emit("dve", lambda e: e.tensor_scalar(out=PC[:, :, 0, 0, :], in0=nsl, scalar1=-1.0, scalar2=None, op0=ALU.mult),
                   reads=ngr, writes=["PC"])
            S.emit("dve", lambda e: e.scalar_tensor_tensor(out=R1[:], in0=nsl, scalar=-1.0, in1=PC[:, :, 0, 0, :],
                                                           op0=ALU.mult, op1=ALU.subtract),
                   reads=ngr + ["PC"], writes=["R1"])
            S.emit("dve", lambda e: e.tensor_copy(out=PC[:, :, 0, 1, :], in_=R1[:]), reads=["R1"], writes=["PC"])
            S.emit("dve", lambda e: e.tensor_tensor(out=R1[:], in0=R1[:], in1=PC[:, :, 0, 1, :], op=ALU.subtract),
                   reads=["R1", "PC"], writes=["R1"])
            S.emit("dve", lambda e: e.tensor_copy(out=PC[:, :, 0, 2, :], in_=R1[:]), reads=["R1"], writes=["PC"])
            S.emit("dve", lambda e: e.tensor_copy(out=PC[:, :, 1, :, :], in_=PC[:, :, 0, :, :]), reads=["PC"], writes=["PC"])

            for mg in range(2):
                slot, bgr = load_unit(("wq", mg), fs)
                for mm in range(4):
                    m = mg * 4 + mm
                    bank = m % 4

                    def fn(e, mm=mm, slot=slot, bank=bank):
                        last = None
                        for kc in range(KC):
                            base = kc * 512 + mm * 128
                            last = e.matmul(ps[bank][:], lhsT=slot[:, base:base + 128], rhs=xnT[:, kc, :],
                                            start=(kc == 0), stop=(kc == KC - 1))
                        return last
                    S.emit("pe", fn, reads=bgr + XNT, writes=[("ps", bank)])
                    S.emit("act", lambda e, m=m, bank=bank: e.activation(out=wv(m), in_=ps[bank][:], func=AF.Copy,
                                                                         scale=0.125),
                           reads=[("ps", bank)], writes=wblk(m))
            def trc(e):
                last = None
                for r in range(4):
                    last = e.transpose(out=psb[4][:, r * 128:(r + 1) * 128],
                                       in_=PC[:, r, :, :, :].rearrange("p d c h -> p (d c h)"), identity=ident[:])
                return last
            S.emit("pe", trc, reads=["PC", "ident"], writes=[("ps", 4)])
            S.emit("act", lambda e: e.activation(out=CQT[:], in_=psb[4][:, 0:T], func=AF.Copy),
                   reads=[("ps", 4)], writes=["CQT"])

            for hf in range(2):
                slot, bgr = load_unit(("wg", hf), fs)
                for mm in range(4):
                    m = hf * 4 + mm
                    bank = m % 4

                    def fn(e, mm=mm, slot=slot, bank=bank):
                        last = None
                        for kc in range(KC):
                            base = kc * 512 + mm * 128
                            last = e.matmul(ps[bank][:], lhsT=slot[:, base:base + 128], rhs=xnT[:, kc, :],
                                            start=(kc == 0), stop=(kc == KC - 1))
                        return last
                    S.emit("pe", fn, reads=bgr + XNT, writes=[("ps", bank)])
                    sigmoid_chain(ps[bank][:], ("ps", bank), wv(8 + m), wblk(8 + m), m % 2)
            its = [(h, j) for h in range(H) for j in range(nj)]
            STB = (4, 5, 3, 2)
            LAG = 3

            def emit_qk(idx):
                h, j = its[idx]
                hp, m = h % 2, h // 2
                r0 = max(0, j - tq0)
                diag = j >= tq0
                sbk = STB[idx % len(STB)]
                zi = 2 * hp + (m % 2)
                if j == 0:
                    S.emit("dve", lambda e: e.tensor_copy(out=QTz[zi][hp * 64:(hp + 1) * 64, :],
                                                          in_=wv(m)[hp * 64:(hp + 1) * 64, :]),
                           reads=wblk(m), writes=[("QTz", zi)])

                def fn(e):
                    e.matmul(ps[sbk][:, r0 * 128:512], lhsT=KT[:, m, j * 128:(j + 1) * 128],
                             rhs=QTz[zi][:, r0 * 128:512], start=True, stop=False)
                    last = e.matmul(ps[sbk][:, r0 * 128:512], lhsT=SELc[:, h:h + 1].to_broadcast([128, 128]),
                                    rhs=CQT[:, r0 * 128:512], start=False, stop=(not diag))
                    if diag:
                        last = e.matmul(ps[sbk][:, r0 * 128:(r0 + 1) * 128], lhsT=ident[:], rhs=maskM[:],
                                        start=False, stop=True)
                    return last
                S.emit("pe", fn, reads=[("KT", j // 4, m), ("QTz", zi), "ident", "maskM", "SEL", "CQT"],
                       writes=[("ps", sbk)])

            deferred = []

            def emit_exp_pv(idx):
                h, j = its[idx]
                hp, m = h % 2, h // 2
                r0 = max(0, j - tq0)
                sbk = STB[idx % len(STB)]
                pblk = 16 + idx % 4
                ob = 6 + h % 2
                S.emit("act", lambda e: e.activation(
                    out=wv(pblk)[:, r0 * 128:512], in_=ps[sbk][:, r0 * 128:512],
                    func=AF.Exp, bias=negc[:, j, h:h + 1], scale=1.0),
                    reads=[("ps", sbk), ("negc", j)], writes=[("work", pblk)])
                S.emit("pe", lambda e: e.matmul(ps[ob][0:DH + 1, r0 * 128:512], lhsT=VA[:, j, h, :],
                                                rhs=wv(pblk)[:, r0 * 128:512], start=(j == 0), stop=(j == nj - 1)),
                       reads=[("work", pblk), ("VA", j, h // 8)], writes=[("ps", ob)])
                if j == nj - 1:
                    rdf = scr[5][64:65, 0:T]
                    o1, o2, o3, o4 = (2, 3, 6, 9) if nj >= 8 else (1, 2, 3, 4)
                    bb = h % 2
                    dve_recip = nj >= 12

                    def st1():
                        if dve_recip:
                            S.emit("dve", lambda e: e.reciprocal(out=rdf, in_=ps[ob][64:65, :]),
                                   reads=[("ps", ob)], writes=[("scr", 5)])
                        else:
                            S.emit("act", lambda e: e.activation(out=rdf, in_=ps[ob][64:65, :], func=AF.Ln),
                                   reads=[("ps", ob)], writes=[("scr", 5)])

                    def st2():
                        if not dve_recip:
                            S.emit("act", lambda e: e.activation(out=rdf, in_=rdf, func=AF.Exp, scale=-1.0),
                                   reads=[("scr", 5)], writes=[("scr", 5)])
                        S.emit("dve", lambda e: e.tensor_copy(out=RD[64:65, 0:T], in_=rdf),
                               reads=[("scr", 5)], writes=["RD"])
                        S.emit("dve", lambda e: e.tensor_tensor(out=RD[64:65, T:2 * T], in0=rdf, in1=RD[64:65, 0:T],
                                                                op=ALU.subtract),
                               reads=[("scr", 5), "RD"], writes=["RD"])

                    dst = wv(8 + m)[hp * 64:(hp + 1) * 64, :]

                    def tail1():
                        def fnb(e):
                            e.matmul(ps[bb][0:64, :], lhsT=SELB[64:65, :], rhs=RD[64:65, 0:T], start=True, stop=False)
                            return e.matmul(ps[bb][0:64, :], lhsT=SELB[64:65, :], rhs=RD[64:65, T:2 * T],
                                            start=False, stop=True)
                        S.emit("pe", fnb, reads=["RD", "SELB"], writes=[("ps", bb)])
                        S.emit("dve", lambda e: e.tensor_copy(out=scr[2][0:64, 0:T], in_=ps[bb][0:64, :]),
                               reads=[("ps", bb)], writes=[("scr", 2)])
                        S.emit("dve", lambda e: e.tensor_tensor(out=scr[3][0:64, 0:T], in0=ps[ob][0:64, :],
                                                                in1=scr[2][0:64, 0:T], op=ALU.mult),
                               reads=[("ps", ob), ("scr", 2)], writes=[("scr", 3)])
                        if hp == 1:
                            S.emit("dve", lambda e: e.tensor_copy(out=scr[4][64:128, 0:T], in_=scr[3][0:64, 0:T]),
                                   reads=[("scr", 3)], writes=[("scr", 4)])

                    def tail2():
                        if hp == 0:
                            S.emit("dve", lambda e: e.tensor_tensor(out=dst, in0=scr[3][0:64, 0:T], in1=dst, op=ALU.mult),
                                   reads=[("scr", 3)] + wblk(8 + m), writes=wblk(8 + m))
                        else:
                            S.emit("dve", lambda e: e.tensor_tensor(out=dst, in0=scr[4][64:128, 0:T], in1=dst, op=ALU.mult),
                                   reads=[("scr", 4)] + wblk(8 + m), writes=wblk(8 + m))
                    deferred.append((idx + o1, st1))
                    deferred.append((idx + o2, st2))
                    deferred.append((idx + o3, tail1))
                    deferred.append((idx + o4, tail2))

            for idx in range(min(LAG, len(its))):
                emit_qk(idx)
            for idx in range(len(its)):
                if idx + LAG < len(its):
                    emit_qk(idx + LAG)
                emit_exp_pv(idx)
                while deferred and deferred[0][0] <= idx:
                    deferred.pop(0)[1]()
            while deferred:
                deferred.pop(0)[1]()
            down_proj(lambda m, t: wv(8 + m)[:, t * 128:(t + 1) * 128], lambda t: wblk(8, 8), "wo", fs, after_t)

        STAGES = [("ffn", 0, 0, 1, True), ("conv", None, 2, 3, False), ("ffn", 1, 4, 5, True),
                  ("kv", None, 6, None, None), ("ffn", 2, 7, 8, True), ("attn", None, 9, 10, False),
                  ("ffn", 3, 11, 12, True)][:stop_stage]
        NS = len(STAGES)

        def load_x(ti):
            p = ti % 2
            S.dma("sp", lambda e: e.dma_start(out=Xb[p][:], in_=x_d[ti * T:(ti + 1) * T, :].rearrange("(t p) d -> p t d", p=128)),
                  [], [("X", p, t) for t in range(4)], f"xin{p}")

        load_x(0)
        load_gain(gpre, "gpre", STAGES[0][2], "gpre")
        for t in range(4):
            norm_A(Xb[0], 0, t)
            norm_B(t)
        for ti in range(ntiles_total):
            p = ti % 2
            Xc = Xb[p]
            tis = ti % tps
            fs = use_scratch and ti > 0
            for si, (kind, f, gi_pre, gi_post, half) in enumerate(STAGES):
                last_stage = si == NS - 1
                has_next = (not last_stage) or (ti + 1 < ntiles_total)
                if last_stage:
                    nXc, nxp, ngi = Xb[(ti + 1) % 2], (ti + 1) % 2, STAGES[0][2]
                else:
                    nXc, nxp, ngi = Xc, p, STAGES[si + 1][2]
                if has_next:
                    load_gain(gpre, "gpre", ngi, "gpre")
                if gi_post is not None:
                    load_gain(gpost, "gpost", gi_post, "gpost")
                if si == min(1, NS - 1) and ti + 1 < ntiles_total:
                    load_x(ti + 1)
                inter = has_next and not last_stage

                def after_t(t, Xc=Xc, p=p, half=half, inter=inter, nXc=nXc, nxp=nxp):
                    if inter and t >= 1:
                        epilogue_t(Xc, p, half, t,
                                   hook1=lambda: norm_A1(nXc, nxp, t - 1),
                                   hook2=lambda: norm_A2(nXc, nxp, t - 1))
                    else:
                        epilogue_t(Xc, p, half, t)
                    if inter:
                        if t >= 2:
                            norm_B(t - 2)
                        if t == 3:
                            norm_A(nXc, nxp, 3)
                            norm_B(2)
                            norm_B(3, split=True)

                if kind == "ffn":
                    if last_stage and has_next:
                        ffn_body(f, fs, after_t,
                                 chunk_hook=lambda j: norm_A(nXc, nxp, j - (NJ - 4)) if j >= NJ - 4 else None,
                                 mid_hook=lambda: [norm_B(t) for t in range(4)])
                    else:
                        ffn_body(f, fs, after_t)
                elif kind == "conv":
                    conv_body(tis == 0, fs, after_t)
                elif kind == "kv":
                    if has_next:
                        for t in range(4):
                            norm_A(nXc, nxp, t)
                    kv_body(tis, fs)
                    if has_next:
                        for t in range(4):
                            norm_B(t)
                elif kind == "attn":
                    attn_body(tis, fs, after_t)
            if ti == 0:
                emit_conv(10 ** 6)
            S.dma("sp", lambda e, ti=ti, p=p: e.dma_start(out=y_d[ti * T:(ti + 1) * T, :].rearrange("(t p) d -> p t d", p=128),
                                                         in_=Xb[p][:]),
                  [("X", p, t) for t in range(4)], [], f"yout{p}")
        S.final_wait("sp", ["yout0", "yout1"])

        with nc.Block() as block:
            @block.tensor
            def _(e):
                for op in S.ops["pe"]:
                    op(e)

            @block.scalar
            def _(e):
                for op in S.ops["act"]:
                    op(e)

            @block.vector
            def _(e):
                for op in S.ops["dve"]:
                    op(e)

            @block.gpsimd
            def _(e):
                for op in S.ops["pool"]:
                    op(e)

            @block.sync
            def _(e):
                for op in S.ops["sp"]:
                    op(e)
    return nc


_CACHE = {}


def _weights_map(inp):
    f = lambda a: np.ascontiguousarray(np.asarray(a, dtype=np.float32))
    gains = np.stack([
        inp["ffn1_pre_g"][0], inp["ffn1_post_g"][0], inp["mix_pre_g"][0], inp["mix_post_g"][0],
        inp["ffn2_pre_g"][0], inp["ffn2_post_g"][0], inp["kv_g"],
        inp["ffn1_pre_g"][1], inp["ffn1_post_g"][1], inp["mix_pre_g"][1], inp["mix_post_g"][1],
        inp["ffn2_pre_g"][1], inp["ffn2_post_g"][1]], axis=0)
    kcv = np.asarray(inp["conv_k"][0]).reshape(3, KC, 128).transpose(2, 1, 0).reshape(128, 24)
    return {
        "gains": f(gains), "fb": f(np.asarray(inp["forget_b"]).reshape(1, H)), "kcv": f(kcv),
        "f1wi": f(inp["ffn1_w_in"]), "f1wo": f(inp["ffn1_w_out"]),
        "f2wi": f(inp["ffn2_w_in"]), "f2wo": f(inp["ffn2_w_out"]),
        "cwi": f(inp["conv_w_in"][0]), "cwo": f(inp["conv_w_out"][0]),
        "kvw": f(inp["kv_w"]), "wqg": f(inp["attn_w_qg"][0]), "wo": f(inp["attn_w_o"][0]),
    }


def kernel(**inputs):
    inp = {k: np.asarray(v) for k, v in inputs.items()}
    x = np.ascontiguousarray(inp["x"], dtype=np.float32)
    B, SEQ, _ = x.shape
    ncores = 8
    n_seq = B // ncores
    key = (n_seq, SEQ)
    if key not in _CACHE:
        _CACHE[key] = build_program(n_seq=n_seq, seq=SEQ)
    nc = _CACHE[key]
    wm = _weights_map(inp)
    in_maps = []
    for c in range(ncores):
        m = dict(wm)
        m["x"] = np.ascontiguousarray(x[c * n_seq:(c + 1) * n_seq].reshape(n_seq * SEQ, D))
        in_maps.append(m)
    res = run_bass_kernel_spmd(nc, in_maps, core_ids=list(range(ncores)))
    out = np.concatenate([np.asarray(r["y"]).reshape(n_seq, SEQ, D) for r in res.results], axis=0)
    return out.astype(np.float32)
```

```python
import math
from contextlib import ExitStack

import numpy as np
import concourse.bass as bass
import concourse.mybir as mybir
from concourse.bass_utils import run_bass_kernel_spmd

F32 = mybir.dt.float32
BF16 = mybir.dt.bfloat16
ALU = mybir.AluOpType
AF = mybir.ActivationFunctionType

D = 1024
KC = 8
DFF = 2816
NJ = 22
NG = 11
T = 512
H = 16
DH = 64
EPS = 1e-6
NBIG = 4
NSMALL = 3
NSCR = 6
MASK_NEG = -30000.0
LN_HALF = math.log(0.5)


class Sched:
    ENG = ("pe", "act", "dve", "pool", "sp")

    def __init__(self, nc, stack):
        self.nc = nc
        self.stack = stack
        self.ops = {e: [] for e in self.ENG}
        self.cnt = {e: 0 for e in self.ENG}
        self.sem = {}
        self.dcnt = {}
        self.waited = {e: {} for e in self.ENG}
        self.lastw = {}
        self.readers = {}
        for e in ("pe", "act", "dve", "pool"):
            self.getsem(e)

    def getsem(self, name):
        if name not in self.sem:
            self.sem[name] = self.stack.enter_context(self.nc.semaphore("s_" + name))
        return self.sem[name]

    def _deps(self, eng, reads, writes):
        need = {}

        def add(tok, raw):
            if tok is None:
                return
            s, v = tok
            if s == eng and (eng == "pe" or not raw):
                return
            if v > need.get(s, 0):
                need[s] = v

        for g in reads:
            add(self.lastw.get(g), True)
        for g in writes:
            add(self.lastw.get(g), False)
            for s, v in self.readers.get(g, {}).items():
                add((s, v), False)
        out = []
        for s, v in need.items():
            if self.waited[eng].get(s, 0) < v:
                self.waited[eng][s] = v
                out.append((self.sem[s], v))
        return out

    def _commit(self, tok, reads, writes):
        for g in writes:
            self.lastw[g] = tok
            self.readers[g] = {}
        for g in reads:
            r = self.readers.setdefault(g, {})
            if tok[1] > r.get(tok[0], 0):
                r[tok[0]] = tok[1]

    def emit(self, eng, fn, reads=(), writes=()):
        waits = self._deps(eng, reads, writes)
        self.cnt[eng] += 1
        tok = (eng, self.cnt[eng])
        sem = self.sem[eng]

        def op(e, fn=fn, waits=waits, sem=sem):
            for s, v in waits:
                e.wait_ge(s, v)
            fn(e).then_inc(sem, 1)

        self.ops[eng].append(op)
        self._commit(tok, reads, writes)

    def dma(self, q, fn, reads, writes, semname):
        sem = self.getsem(semname)
        waits = [(s_, v_) for (s_, v_) in self._deps(q, reads, writes) if s_ is not sem]
        self.dcnt[semname] = self.dcnt.get(semname, 0) + 16
        tok = (semname, self.dcnt[semname])

        def op(e, fn=fn, waits=waits, sem=sem):
            for s, v in waits:
                e.wait_ge(s, v)
            fn(e).then_inc(sem, 16)

        self.ops[q].append(op)
        self._commit(tok, reads, writes)

    def final_wait(self, q, semnames):
        items = [(self.sem[n], self.dcnt[n]) for n in semnames if n in self.dcnt]

        def op(e, items=items):
            for s, v in items:
                e.wait_ge(s, v)

        self.ops[q].append(op)


def build_program(n_seq=2, seq=2048, stop_stage=7):
    nc = bass.Bass("TRN2", target_bir_lowering=False)
    ntok = n_seq * seq
    tps = seq // T
    ntiles_total = n_seq * tps
    NQ = seq // 128

    def din(name, shape):
        return nc.dram_tensor(name, list(shape), F32, kind="ExternalInput").ap()

    x_d = din("x", [ntok, D])
    y_d = nc.dram_tensor("y", [ntok, D], F32, kind="ExternalOutput").ap()
    G_d = din("gains", [13, D])
    fb_d = din("fb", [1, H])
    kcv_d = din("kcv", [128, 24])
    f1wi = din("f1wi", [2, D, 2 * DFF])
    f1wo = din("f1wo", [2, DFF, D])
    f2wi = din("f2wi", [2, D, 2 * DFF])
    f2wo = din("f2wo", [2, DFF, D])
    cwi = din("cwi", [D, 3 * D])
    cwo = din("cwo", [D, D])
    kvw = din("kvw", [D, 2 * D + H])
    wqg = din("wqg", [D, 2 * D])
    wo_d = din("wo", [D, D])

    with ExitStack() as st:
        def sb(name, shape, dt):
            return st.enter_context(nc.sbuf_tensor(name, list(shape), dt))

        Xb = [sb(f"Xb{i}", [128, 4, D], F32) for i in range(2)]
        KT = sb("KT", [128, KC, seq], BF16)
        VA = sb("VA", [128, NQ, H, DH + 1], BF16)
        spt = sb("spt", [128, NQ, H], F32)
        negc = sb("negc", [128, NQ, H], F32)
        carry = sb("carry", [128, H], F32)
        xn = [sb(f"xn{i}", [128, D], BF16) for i in range(4)]
        xnT = sb("xnT", [128, KC, T], BF16)
        work = sb("work", [128, 22 * 512], BF16)
        big = [sb(f"big{i}", [128, 4096], BF16) for i in range(NBIG)]
        small = [sb(f"small{i}", [128, 1024], BF16) for i in range(NSMALL)]
        gpre = sb("gpre", [128, D], F32)
        gpost = sb("gpost", [128, D], F32)
        scr = [sb(f"scr{i}", [128, 520], F32) for i in range(NSCR)]
        junk = sb("junk", [128, D], BF16)
        stt = sb("stt", [128, 80], F32)
        SELc = sb("SELc", [128, H], BF16)
        QTz = [sb(f"QTz{i}", [128, T], BF16) for i in range(4)]
        PC = sb("PC", [128, 4, 2, 4, H], BF16)
        R1 = sb("R1", [128, 4, H], F32)
        CQT = sb("CQT", [128, T], BF16)
        RD = sb("RD", [128, 2 * T], BF16)
        SELB = sb("SELB", [128, 64], BF16)
        halo = sb("halo", [128, KC, 2], F32)
        kcv = sb("kcv_sb", [128, 24], F32)
        fb = sb("fb_sb", [128, H], F32)
        zf = sb("zf", [128, H], F32)
        ident = sb("ident", [128, 128], BF16)
        maskM = sb("maskM", [128, 128], BF16)
        Umat = sb("Umat", [128, 128], F32)
        ONES = sb("ONES", [128, 128], F32)
        ps = [st.enter_context(nc.psum_tensor(f"ps{i}", [128, 512], F32)) for i in range(8)]
        psb = [p.bitcast(BF16) for p in ps]

        S = Sched(nc, st)
        state = {"big": 0, "small": 0, "st": 0, "bank": 0}

        def wblk(b0, n=1):
            return [("work", b) for b in range(b0, b0 + n)]

        def wv(b0, n=1):
            return work[:, b0 * 512:(b0 + n) * 512]

        def stcols(n):
            c = state["st"]
            if c + n > 80:
                c = 0
            state["st"] = c + n
            return c

        S.emit("pool", lambda e: e.memset(ident[:], 0.0), writes=["ident"])
        S.emit("pool", lambda e: e.affine_select(out=ident[:], in_=ident[:], pattern=[[-1, 128]],
                                                 compare_op=ALU.not_equal, fill=1.0, base=0,
                                                 channel_multiplier=1),
               reads=["ident"], writes=["ident"])
        S.emit("pool", lambda e: e.memset(maskM[:], 0.0), writes=["maskM"])
        S.emit("pool", lambda e: e.affine_select(out=maskM[:], in_=maskM[:], pattern=[[1, 128]],
                                                 compare_op=ALU.is_ge, fill=MASK_NEG, base=0,
                                                 channel_multiplier=-1),
               reads=["maskM"], writes=["maskM"])
        S.emit("pool", lambda e: e.memset(Umat[:], 1.0), writes=["Umat"])
        S.emit("pool", lambda e: e.affine_select(out=Umat[:], in_=Umat[:], pattern=[[1, 128]],
                                                 compare_op=ALU.is_ge, fill=0.0, base=0,
                                                 channel_multiplier=-1),
               reads=["Umat"], writes=["Umat"])
        S.emit("pool", lambda e: e.memset(ONES[:], 1.0), writes=["ONES"])
        S.emit("pool", lambda e: e.memset(SELc[:], 0.0), writes=["SEL"])
        for off in (0, 16, 32):
            S.emit("pool", lambda e, off=off: e.affine_select(out=SELc[:], in_=SELc[:], pattern=[[-1, H]],
                                                             compare_op=ALU.not_equal, fill=1.0, base=-off,
                                                             channel_multiplier=1),
                   reads=["SEL"], writes=["SEL"])
        for i in range(4):
            S.emit("pool", lambda e, i=i: e.memset(QTz[i][:], 0.0), writes=[("QTz", i)])
        S.emit("pool", lambda e: e.memset(PC[:], 0.0), writes=["PC"])
        S.emit("pool", lambda e: e.memset(RD[:], 0.0), writes=["RD"])
        S.emit("pool", lambda e: e.memset(SELB[:], 0.0), writes=["SELB"])
        S.emit("pool", lambda e: e.memset(SELB[64:65, :], 1.0), writes=["SELB"])
        S.emit("pool", lambda e: e.memset(SELB[96:97, :], 1.0), writes=["SELB"])
        S.emit("pool", lambda e: e.memset(VA[:, :, :, DH:DH + 1], 1.0),
               writes=[("VA", q, hf) for q in range(NQ) for hf in range(2)])
        S.dma("sp", lambda e: e.dma_start(out=kcv[:], in_=kcv_d), [], ["kcv"], "cst0")
        S.dma("sp", lambda e: e.dma_start(out=fb[:], in_=fb_d.broadcast_to([128, H])), [], ["fb"], "cst1")

        units = {}
        ulist = []

        def mk(key, n, parts, ring):
            units[key] = dict(idx=len(ulist), n=n, parts=parts, ring=ring)
            ulist.append(key)

        def kcp(ap):
            return ap.rearrange("(kc p) n -> p kc n", p=128)

        def cpn(ap):
            return ap.rearrange("(c p) n -> p c n", p=128)

        def v3(ap, a):
            return ap.rearrange("p (a b) -> p a b", a=a)

        def sl3(a, b, k):
            return lambda s: v3(s[:, a:b], k)

        ffn_w = [(f1wi[0], f1wo[0]), (f2wi[0], f2wo[0]), (f1wi[1], f1wo[1]), (f2wi[1], f2wo[1])]

        def mk_ffn(f):
            wi, wo = ffn_w[f]
            for g in range(NG):
                mk(("wi", f, g), 4096, [(sl3(0, 2048, KC), kcp(wi[:, g * 256:(g + 1) * 256])),
                                        (sl3(2048, 4096, KC), kcp(wi[:, DFF + g * 256:DFF + (g + 1) * 256]))], "big")
                mk(("woA", f, g), 1024, [(sl3(0, 1024, 2), cpn(wo[g * 256:(g + 1) * 256, 0:512]))], "small")
            for gb in range(3):
                nch = min(8, NJ - 8 * gb)
                mk(("woB", f, gb), nch * 512, [(sl3(0, nch * 512, nch), cpn(wo[gb * 1024:gb * 1024 + nch * 128, 512:1024]))], "big")

        mk_ffn(0)
        for m in range(KC):
            mk(("cwi", m), 3072, [(sl3(i * 1024, (i + 1) * 1024, KC), kcp(cwi[:, i * D + m * 128:i * D + (m + 1) * 128]))
                                  for i in range(3)], "big")
        for hh in range(2):
            mk(("cwo", hh), 4096, [(sl3(0, 4096, 4), cpn(cwo[hh * 512:(hh + 1) * 512, :]))], "big")
        mk_ffn(1)
        for mg in range(2):
            mk(("kvk", mg), 4096, [(sl3(0, 4096, KC), kcp(kvw[:, mg * 512:(mg + 1) * 512]))], "big")
        for hf in range(2):
            mk(("kvv", hf), 4096, [(sl3(0, 4096, KC), kcp(kvw[:, D + hf * 512:D + (hf + 1) * 512]))], "big")
        mk(("kvf",), 128, [(sl3(0, 128, KC), kcp(kvw[:, 2 * D:2 * D + H]))], "small")
        mk_ffn(2)
        for mg in range(2):
            mk(("wq", mg), 4096, [(sl3(0, 4096, KC), kcp(wqg[:, mg * 512:(mg + 1) * 512]))], "big")
        for hf in range(2):
            mk(("wg", hf), 4096, [(sl3(0, 4096, KC), kcp(wqg[:, D + hf * 512:D + (hf + 1) * 512]))], "big")
        for hh in range(2):
            mk(("wo", hh), 4096, [(sl3(0, 4096, 4), cpn(wo_d[hh * 512:(hh + 1) * 512, :]))], "big")
        mk_ffn(3)

        use_scratch = ntiles_total > 1
        if use_scratch:
            sc = nc.dram_tensor("wscratch", [len(ulist), 128, 4096], BF16).ap()
        cv = {"list": [], "pos": 0}
        written_back = set()

        def emit_conv(n):
            if not use_scratch:
                return
            while n > 0 and cv["pos"] < len(cv["list"]):
                key, i = cv["list"][cv["pos"]]
                cv["pos"] += 1
                n -= 1
                u = units[key]
                dfn, src = u["parts"][i]
                dst = dfn(sc[u["idx"]])
                S.dma("pool", lambda e, dst=dst, src=src: e.dma_start(out=dst, in_=src), [], ["cvall"], "cv")

        def load_gain(dst, gran, idx, semname):
            S.dma("sp", lambda e: e.dma_start(out=dst[:], in_=G_d[idx:idx + 1, :].broadcast_to([128, D])),
                  [], [gran], semname)

        def load_unit(key, from_scratch):
            u = units[key]
            if u["ring"] == "big":
                k = state["big"] % NBIG
                state["big"] += 1
                slot, gran, semname = big[k], ("big", k), f"big{k}"
            else:
                k = state["small"] % NSMALL
                state["small"] += 1
                slot, gran, semname = small[k], ("small", k), f"small{k}"
            if from_scratch:
                n = u["n"]
                src = sc[u["idx"], :, 0:n]
                S.dma("pool", lambda e, slot=slot, src=src, n=n: e.dma_start(out=slot[:, 0:n], in_=src),
                      [("sc", key)], [gran], semname)
            else:
                for dfn, src in u["parts"]:
                    S.dma("pool", lambda e, dfn=dfn, src=src, slot=slot: e.dma_start(out=dfn(slot), in_=src),
                          [], [gran], semname)
                if use_scratch and key not in written_back:
                    written_back.add(key)
                    n = u["n"]
                    dst = sc[u["idx"], :, 0:n]
                    S.dma("sp", lambda e, slot=slot, dst=dst, n=n: e.dma_start(out=dst, in_=slot[:, 0:n]),
                          [gran], [("sc", key)], "wb_" + semname)
            return slot, [gran]

        nst = {}

        def norm_A1(Xc, xp, t):
            c = stcols(3)
            nst[t] = c
            S.emit("act", lambda e: e.activation(out=junk[:], in_=Xc[:, t, :], func=AF.Square,
                                                 scale=1.0 / 32.0, accum_out=stt[:, c:c + 1]),
                   reads=[("X", xp, t)], writes=["junk", ("st", c)])
            S.emit("act", lambda e: e.activation(out=stt[:, c + 1:c + 2], in_=stt[:, c:c + 1], func=AF.Ln, bias=EPS),
                   reads=[("st", c)], writes=[("st", c + 1)])
            S.emit("act", lambda e: e.activation(out=stt[:, c + 2:c + 3], in_=stt[:, c + 1:c + 2], func=AF.Exp, scale=-0.5),
                   reads=[("st", c + 1)], writes=[("st", c + 2)])

        def norm_A2(Xc, xp, t):
            c = nst.pop(t)
            S.emit("dve", lambda e: e.scalar_tensor_tensor(out=xn[t][:], in0=Xc[:, t, :], scalar=stt[:, c + 2:c + 3],
                                                           in1=gpre[:], op0=ALU.mult, op1=ALU.mult),
                   reads=[("X", xp, t), ("st", c + 2), "gpre"], writes=[("xn", t)])

        def norm_A(Xc, xp, t):
            norm_A1(Xc, xp, t)
            norm_A2(Xc, xp, t)

        def norm_B(t, split=False):
            def tr(e):
                last = None
                for m in range(KC):
                    last = e.transpose(out=psb[t][:, m * 128:(m + 1) * 128],
                                       in_=xn[t][:, m * 128:(m + 1) * 128], identity=ident[:])
                return last
            S.emit("pe", tr, reads=[("xn", t), "ident"], writes=[("ps", t)])
            if split:
                S.emit("act", lambda e: e.activation(out=xnT[:, 0:4, t * 128:(t + 1) * 128],
                                                     in_=v3(psb[t][:, 0:512], 4), func=AF.Copy),
                       reads=[("ps", t)], writes=[("xnT", t)])
                S.emit("dve", lambda e: e.tensor_copy(out=xnT[:, 4:8, t * 128:(t + 1) * 128],
                                                      in_=v3(psb[t][:, 512:1024], 4)),
                       reads=[("ps", t)], writes=[("xnT", t)])
            else:
                S.emit("act", lambda e: e.activation(out=xnT[:, :, t * 128:(t + 1) * 128],
                                                     in_=v3(psb[t][:, :], KC), func=AF.Copy),
                       reads=[("ps", t)], writes=[("xnT", t)])

        XNT = [("xnT", t) for t in range(4)]

        sqA = {}

        def epi_sqA(t):
            c = stcols(5)
            sqA[t] = c
            S.emit("act", lambda e: e.activation(out=junk[:, 0:512], in_=ps[4 + t][:], func=AF.Square,
                                                 scale=1.0 / 32.0, accum_out=stt[:, c:c + 1]),
                   reads=[("ps", 4 + t)], writes=["junk", ("st", c)])

        def epilogue_t(Xc, xp, half, t, hook1=None, hook2=None):
            if t not in sqA:
                epi_sqA(t)
            c = sqA.pop(t)
            S.emit("act", lambda e: e.activation(out=junk[:, 512:1024], in_=ps[t][:], func=AF.Square,
                                                 scale=1.0 / 32.0, accum_out=stt[:, c + 1:c + 2]),
                   reads=[("ps", t)], writes=["junk", ("st", c + 1)])
            if hook1 is not None:
                hook1()
            S.emit("dve", lambda e: e.tensor_tensor(out=stt[:, c + 2:c + 3], in0=stt[:, c:c + 1],
                                                    in1=stt[:, c + 1:c + 2], op=ALU.add),
                   reads=[("st", c), ("st", c + 1)], writes=[("st", c + 2)])

            S.emit("act", lambda e: e.activation(out=stt[:, c + 3:c + 4], in_=stt[:, c + 2:c + 3], func=AF.Ln, bias=EPS),
                   reads=[("st", c + 2)], writes=[("st", c + 3)])
            S.emit("act", lambda e: e.activation(out=stt[:, c + 4:c + 5], in_=stt[:, c + 3:c + 4], func=AF.Exp,
                                                 scale=-0.5, bias=(LN_HALF if half else 0.0)),
                   reads=[("st", c + 3)], writes=[("st", c + 4)])
            for hf, bank in enumerate((4 + t, t)):
                S.emit("dve", lambda e, hf=hf, bank=bank: e.scalar_tensor_tensor(
                    out=scr[4 + hf][:, 0:512], in0=ps[bank][:], scalar=stt[:, c + 4:c + 5],
                    in1=gpost[:, hf * 512:(hf + 1) * 512], op0=ALU.mult, op1=ALU.mult),
                    reads=[("ps", bank), ("st", c + 4), "gpost"], writes=[("scr", 4 + hf)])
            for hf in range(2):
                S.emit("dve", lambda e, hf=hf: e.tensor_tensor(
                    out=Xc[:, t, hf * 512:(hf + 1) * 512], in0=scr[4 + hf][:, 0:512],
                    in1=Xc[:, t, hf * 512:(hf + 1) * 512], op=ALU.add),
                    reads=[("scr", 4 + hf), ("X", xp, t)], writes=[("X", xp, t)])
            if hook2 is not None:
                hook2()

        def sigmoid_chain(src_bank_ap, src_gran, dst_ap, dst_gran, si):
            tmp = scr[si][:, 0:512]
            S.emit("act", lambda e: e.activation(out=tmp, in_=src_bank_ap, func=AF.Exp, scale=-1.0),
                   reads=[src_gran], writes=[("scr", si)])
            S.emit("act", lambda e: e.activation(out=tmp, in_=tmp, func=AF.Ln, bias=1.0),
                   reads=[("scr", si)], writes=[("scr", si)])
            S.emit("act", lambda e: e.activation(out=dst_ap, in_=tmp, func=AF.Exp, scale=-1.0),
                   reads=[("scr", si)], writes=dst_gran)

        def ffn_body(f, fs, after_t, chunk_hook=None, mid_hook=None):
            pend = None

            def emit_DA(j, sslot, sgr, jj):
                def fn(e):
                    last = None
                    for t in range(4):
                        last = e.matmul(ps[4 + t][:], lhsT=wv(j)[:, t * 128:(t + 1) * 128],
                                        rhs=sslot[:, jj * 512:(jj + 1) * 512],
                                        start=(j == 0), stop=(j == NJ - 1))
                    return last
                S.emit("pe", fn, reads=wblk(j) + sgr, writes=[("ps", 4 + t) for t in range(4)])

            for g in range(NG):
                slot, bgr = load_unit(("wi", f, g), fs)
                sslot, sgr = load_unit(("woA", f, g), fs)
                for jj in range(2):
                    j = 2 * g + jj
                    pg, pu = (0, 1) if j % 2 == 0 else (2, 3)
                    for which, bank in ((0, pg), (1, pu)):
                        def fn(e, which=which, bank=bank, jj=jj, slot=slot):
                            last = None
                            for kc in range(KC):
                                base = which * 2048 + kc * 256 + jj * 128
                                last = e.matmul(ps[bank][:], lhsT=slot[:, base:base + 128], rhs=xnT[:, kc, :],
                                                start=(kc == 0), stop=(kc == KC - 1))
                            return last
                        S.emit("pe", fn, reads=bgr + XNT, writes=[("ps", bank)])
                    if pend is not None:
                        emit_DA(*pend)
                    si = j % 2
                    sigmoid_chain(ps[pg][:], ("ps", pg), scr[si][:, 0:512], [("scr", si)], si)
                    S.emit("dve", lambda e, si=si, pg=pg: e.tensor_tensor(out=scr[2 + si][:, 0:512], in0=ps[pg][:],
                                                                          in1=scr[si][:, 0:512], op=ALU.mult),
                           reads=[("ps", pg), ("scr", si)], writes=[("scr", 2 + si)])
                    S.emit("dve", lambda e, si=si, pu=pu, j=j: e.tensor_tensor(out=wv(j), in0=ps[pu][:],
                                                                              in1=scr[2 + si][:, 0:512], op=ALU.mult),
                           reads=[("ps", pu), ("scr", 2 + si)], writes=wblk(j))
                    pend = (j, sslot, sgr, jj)
                    if chunk_hook is not None:
                        chunk_hook(j)
            bslots = None

            def emit_B(t, gb):
                nch = min(8, NJ - 8 * gb)
                slot, bgr = bslots[gb]

                def fn(e, t=t, gb=gb, nch=nch, slot=slot):
                    last = None
                    for c_ in range(nch):
                        j = 8 * gb + c_
                        last = e.matmul(ps[t][:], lhsT=wv(j)[:, t * 128:(t + 1) * 128],
                                        rhs=slot[:, c_ * 512:(c_ + 1) * 512],
                                        start=(j == 0), stop=(j == NJ - 1))
                    return last
                S.emit("pe", fn, reads=wblk(8 * gb, nch) + bgr, writes=[("ps", t)])

            early = []
            if mid_hook is None:
                bslots = [load_unit(("woB", f, gb), fs) for gb in range(3)]
                for gb in range(2):
                    emit_B(0, gb)
                    early.append((0, gb))
            emit_DA(*pend)
            if mid_hook is not None:
                mid_hook()
            for t in range(4):
                epi_sqA(t)
            if bslots is None:
                bslots = [load_unit(("woB", f, gb), fs) for gb in range(3)]
            for t in range(4):
                for gb in range(3):
                    if (t, gb) not in early:
                        emit_B(t, gb)
                after_t(t)

        def down_proj(lhs_fn, lhs_gran, key, fs, after_t):
            slots = [load_unit((key, hh), fs) for hh in range(2)]
            sgr = slots[0][1] + slots[1][1]

            def part(t, hf, bank, ms):
                ms = list(ms)

                def fn(e):
                    last = None
                    for m in ms:
                        slot = slots[m // 4][0]
                        base = (m % 4) * 1024 + hf * 512
                        last = e.matmul(ps[bank][:], lhsT=lhs_fn(m, t), rhs=slot[:, base:base + 512],
                                        start=(m == 0), stop=(m == KC - 1))
                    return last
                S.emit("pe", fn, reads=[g for m in ms for g in lhs_gran(m)] + sgr, writes=[("ps", bank)])

            for t in (0, 1):
                for hf, bank in enumerate((4 + t, t)):
                    part(t, hf, bank, range(KC - 1))
            for t in (0, 1):
                for hf, bank in enumerate((4 + t, t)):
                    part(t, hf, bank, [KC - 1])
                after_t(t)
            for t in (2, 3):
                for hf, bank in enumerate((4 + t, t)):
                    part(t, hf, bank, range(KC))
                after_t(t)

        def conv_body(first_tile, fs, after_t):
            if first_tile:
                S.emit("dve", lambda e: e.memset(halo[:], 0.0), writes=[("halo", m) for m in range(KC)])
            for m in range(KC):
                slot, bgr = load_unit(("cwi", m), fs)
                banks = (0, 1, 2) if m % 2 == 0 else (3, 4, 5)
                for i in range(3):
                    def fn(e, i=i, slot=slot, bank=banks[i]):
                        last = None
                        for kc in range(KC):
                            base = i * 1024 + kc * 128
                            last = e.matmul(ps[bank][:], lhsT=slot[:, base:base + 128], rhs=xnT[:, kc, :],
                                            start=(kc == 0), stop=(kc == KC - 1))
                        return last
                    S.emit("pe", fn, reads=bgr + XNT, writes=[("ps", banks[i])])
                pb_, pc_, ph_ = banks
                ci = m % 2
                ui = 2 + m % 2
                yi = 4 + m % 2
                S.emit("act", lambda e, ci=ci, pc_=pc_: e.activation(out=scr[ci][:, 0:512], in_=ps[pc_][:], func=AF.Copy),
                       reads=[("ps", pc_)], writes=[("scr", ci)])
                S.emit("act", lambda e, ui=ui, m=m: e.activation(out=scr[ui][:, 0:2], in_=halo[:, m, :], func=AF.Copy),
                       reads=[("halo", m)], writes=[("scr", ui)])
                S.emit("dve", lambda e, ui=ui, ci=ci, ph_=ph_: e.tensor_tensor(out=scr[ui][:, 2:514], in0=ps[ph_][:],
                                                                                in1=scr[ci][:, 0:512], op=ALU.mult),
                       reads=[("ps", ph_), ("scr", ci), ("scr", ui)], writes=[("scr", ui)])
                S.emit("act", lambda e, ui=ui, m=m: e.activation(out=halo[:, m, :], in_=scr[ui][:, 512:514], func=AF.Copy),
                       reads=[("scr", ui)], writes=[("halo", m)])
                S.emit("dve", lambda e, ui=ui, yi=yi, m=m: e.tensor_scalar(
                    out=scr[yi][:, 0:512], in0=scr[ui][:, 2:514], scalar1=kcv[:, m * 3 + 2:m * 3 + 3],
                    scalar2=None, op0=ALU.mult),
                    reads=[("scr", ui), "kcv"], writes=[("scr", yi)])
                for w_, off in ((1, 1), (0, 0)):
                    S.emit("dve", lambda e, ui=ui, yi=yi, m=m, w_=w_, off=off: e.scalar_tensor_tensor(
                        out=scr[yi][:, 0:512], in0=scr[ui][:, off:off + 512], scalar=kcv[:, m * 3 + w_:m * 3 + w_ + 1],
                        in1=scr[yi][:, 0:512], op0=ALU.mult, op1=ALU.add),
                        reads=[("scr", ui), ("scr", yi), "kcv"], writes=[("scr", yi)])
                S.emit("dve", lambda e, yi=yi, m=m, pb_=pb_: e.tensor_tensor(out=wv(m), in0=ps[pb_][:],
                                                                             in1=scr[yi][:, 0:512], op=ALU.mult),
                       reads=[("ps", pb_), ("scr", yi)], writes=wblk(m))
            down_proj(lambda m, t: wv(m)[:, t * 128:(t + 1) * 128], lambda m: wblk(m), "cwo", fs, after_t)

        def kv_body(tile_in_seq, fs):
            tq0 = tile_in_seq * 4
            if tile_in_seq == 0:
                S.emit("dve", lambda e: e.memset(carry[:], 0.0), writes=["carry"])
            fslot, fgr = load_unit(("kvf",), fs)
            for t in range(4):
                bank = 4 + t

                def fn(e, t=t, bank=bank):
                    last = None
                    for kc in range(KC):
                        last = e.matmul(ps[bank][:, 0:H], lhsT=xnT[:, kc, t * 128:(t + 1) * 128],
                                        rhs=fslot[:, kc * H:(kc + 1) * H],
                                        start=(kc == 0), stop=(kc == KC - 1))
                    return last
                S.emit("pe", fn, reads=fgr + [("xnT", t)], writes=[("ps", bank)])
                S.emit("dve", lambda e, bank=bank: e.tensor_tensor(out=zf[:], in0=ps[bank][:, 0:H], in1=fb[:], op=ALU.add),
                       reads=[("ps", bank), "fb"], writes=["zf"])
                S.emit("act", lambda e: e.activation(out=zf[:], in_=zf[:], func=AF.Exp, scale=-1.0),
                       reads=["zf"], writes=["zf"])
                S.emit("act", lambda e, t=t: e.activation(out=spt[:, tq0 + t, :], in_=zf[:], func=AF.Ln, bias=1.0),
                       reads=["zf"], writes=[("spt", tq0 + t)])
            for mg in range(2):
                slot, bgr = load_unit(("kvk", mg), fs)
                for mm in range(4):
                    m = mg * 4 + mm
                    bank = m % 4

                    def fn(e, mm=mm, slot=slot, bank=bank):
                        last = None
                        for kc in range(KC):
                            base = kc * 512 + mm * 128
                            last = e.matmul(ps[bank][:], lhsT=slot[:, base:base + 128], rhs=xnT[:, kc, :],
                                            start=(kc == 0), stop=(kc == KC - 1))
                        return last
                    S.emit("pe", fn, reads=bgr + XNT, writes=[("ps", bank)])
                    if m % 2 == 0:
                        S.emit("act", lambda e, m=m, bank=bank: e.activation(
                            out=KT[:, m, tile_in_seq * T:(tile_in_seq + 1) * T], in_=ps[bank][:], func=AF.Copy),
                            reads=[("ps", bank)], writes=[("KT", tile_in_seq, m)])
                    else:
                        S.emit("dve", lambda e, m=m, bank=bank: e.tensor_copy(
                            out=KT[:, m, tile_in_seq * T:(tile_in_seq + 1) * T], in_=ps[bank][:]),
                            reads=[("ps", bank)], writes=[("KT", tile_in_seq, m)])
            sgr = [("spt", tq0 + t) for t in range(4)]
            S.emit("pe", lambda e: e.matmul(ps[4][:, 0:4 * H], lhsT=Umat[:], rhs=spt[:, tq0:tq0 + 4, :],
                                            start=True, stop=True),
                   reads=sgr + ["Umat"], writes=[("ps", 4)])
            S.emit("pe", lambda e: e.matmul(ps[5][:, 0:4 * H], lhsT=ONES[:], rhs=spt[:, tq0:tq0 + 4, :],
                                            start=True, stop=True),
                   reads=sgr + ["ONES"], writes=[("ps", 5)])
            for t in range(4):
                S.emit("dve", lambda e, t=t: e.tensor_tensor(out=negc[:, tq0 + t, :], in0=ps[4][:, t * H:(t + 1) * H],
                                                             in1=carry[:], op=ALU.add),
                       reads=[("ps", 4), "carry"], writes=[("negc", tq0 + t)])
                S.emit("dve", lambda e, t=t: e.tensor_tensor(out=carry[:], in0=ps[5][:, t * H:(t + 1) * H],
                                                             in1=carry[:], op=ALU.add),
                       reads=[("ps", 5), "carry"], writes=["carry"])
            vslots = [load_unit(("kvv", hf), fs) for hf in range(2)]
            for t in range(4):
                for hf in range(2):
                    bank = 4 + (t * 2 + hf) % 4
                    slot, bgr = vslots[hf]

                    def fn(e, t=t, slot=slot, bank=bank):
                        last = None
                        for kc in range(KC):
                            last = e.matmul(ps[bank][:], lhsT=xnT[:, kc, t * 128:(t + 1) * 128],
                                            rhs=slot[:, kc * 512:(kc + 1) * 512],
                                            start=(kc == 0), stop=(kc == KC - 1))
                        return last
                    S.emit("pe", fn, reads=bgr + [("xnT", t)], writes=[("ps", bank)])
                    dst = VA[:, tq0 + t, hf * 8:(hf + 1) * 8, 0:DH]
                    if (t + hf) % 2 == 0:
                        S.emit("act", lambda e, dst=dst, bank=bank: e.activation(out=dst, in_=v3(ps[bank][:], 8), func=AF.Copy),
                               reads=[("ps", bank)], writes=[("VA", tq0 + t, hf)])
                    else:
                        S.emit("dve", lambda e, dst=dst, bank=bank: e.tensor_copy(out=dst, in_=v3(ps[bank][:], 8)),
                               reads=[("ps", bank)], writes=[("VA", tq0 + t, hf)])

        def attn_body(tile_in_seq, fs, after_t):
            tq0 = tile_in_seq * 4
            nj = tq0 + 4
            nsl = negc[:, tq0:tq0 + 4, :]
            ngr = [("negc", tq0 + r) for r in range(4)]
            S.emit("dve", lambda e: e.tensor_scalar(out=PC[:, :, 0, 0, :], in0=nsl, scalar1=-1.0, scalar2=None, op0=ALU.mult),
                   reads=ngr, writes=["PC"])
            S.emit("dve", lambda e: e.scalar_tensor_tensor(out=R1[:], in0=nsl, scalar=-1.0, in1=PC[:, :, 0, 0, :],
                                                           op0=ALU.mult, op1=ALU.subtract),
                   reads=ngr + ["PC"], writes=["R1"])
            S.emit("dve", lambda e: e.tensor_copy(out=PC[:, :, 0, 1, :], in_=R1[:]), reads=["R1"], writes=["PC"])
            S.emit("dve", lambda e: e.tensor_tensor(out=R1[:], in0=R1[:], in1=PC[:, :, 0, 1, :], op=ALU.subtract),
                   reads=["R1", "PC"], writes=["R1"])
            S.emit("dve", lambda e: e.tensor_copy(out=PC[:, :, 0, 2, :], in_=R1[:]), reads=["R1"], writes=["PC"])
            S.emit("dve", lambda e: e.tensor_copy(out=PC[:, :, 1, :, :], in_=PC[:, :, 0, :, :]), reads=["PC"], writes=["PC"])

            for mg in range(2):
                slot, bgr = load_unit(("wq", mg), fs)
                for mm in range(4):
                    m = mg * 4 + mm
                    bank = m % 4

                    def fn(e, mm=mm, slot=slot, bank=bank):
                        last = None
                        for kc in range(KC):
                            base = kc * 512 + mm * 128
                            last = e.matmul(ps[bank][:], lhsT=slot[:, base:base + 128], rhs=xnT[:, kc, :],
                                            start=(kc == 0), stop=(kc == KC - 1))
                        return last
                    S.emit("pe", fn, reads=bgr + XNT, writes=[("ps", bank)])
                    S.emit("act", lambda e, m=m, bank=bank: e.activation(out=wv(m), in_=ps[bank][:], func=AF.Copy,
                                                                         scale=0.125),
                           reads=[("ps", bank)], writes=wblk(m))
            def trc(e):
                last = None
                for r in range(4):
                    last = e.transpose(out=psb[4][:, r * 128:(r + 1) * 128],
                                       in_=PC[:, r, :, :, :].rearrange("p d c h -> p (d c h)"), identity=ident[:])
                return last
            S.emit("pe", trc, reads=["PC", "ident"], writes=[("ps", 4)])
            S.emit("act", lambda e: e.activation(out=CQT[:], in_=psb[4][:, 0:T], func=AF.Copy),
                   reads=[("ps", 4)], writes=["CQT"])

            for hf in range(2):
                slot, bgr = load_unit(("wg", hf), fs)
                for mm in range(4):
                    m = hf * 4 + mm
                    bank = m % 4

                    def fn(e, mm=mm, slot=slot, bank=bank):
                        last = None
                        for kc in range(KC):
                            base = kc * 512 + mm * 128
                            last = e.matmul(ps[bank][:], lhsT=slot[:, base:base + 128], rhs=xnT[:, kc, :],
                                            start=(kc == 0), stop=(kc == KC - 1))
                        return last
                    S.emit("pe", fn, reads=bgr + XNT, writes=[("ps", bank)])
                    sigmoid_chain(ps[bank][:], ("ps", bank), wv(8 + m), wblk(8 + m), m % 2)
            its = [(h, j) for h in range(H) for j in range(nj)]
            STB = (4, 5, 3, 2)
            LAG = 3

            def emit_qk(idx):
                h, j = its[idx]
                hp, m = h % 2, h // 2
                r0 = max(0, j - tq0)
                diag = j >= tq0
                sbk = STB[idx % len(STB)]
                zi = 2 * hp + (m % 2)
                if j == 0:
                    S.emit("dve", lambda e: e.tensor_copy(out=QTz[zi][hp * 64:(hp + 1) * 64, :],
                                                          in_=wv(m)[hp * 64:(hp + 1) * 64, :]),
                           reads=wblk(m), writes=[("QTz", zi)])

                def fn(e):
                    e.matmul(ps[sbk][:, r0 * 128:512], lhsT=KT[:, m, j * 128:(j + 1) * 128],
                             rhs=QTz[zi][:, r0 * 128:512], start=True, stop=False)
                    last = e.matmul(ps[sbk][:, r0 * 128:512], lhsT=SELc[:, h:h + 1].to_broadcast([128, 128]),
                                    rhs=CQT[:, r0 * 128:512], start=False, stop=(not diag))
                    if diag:
                        last = e.matmul(ps[sbk][:, r0 * 128:(r0 + 1) * 128], lhsT=ident[:], rhs=maskM[:],
                                        start=False, stop=True)
                    return last
                S.emit("pe", fn, reads=[("KT", j // 4, m), ("QTz", zi), "ident", "maskM", "SEL", "CQT"],
                       writes=[("ps", sbk)])

            deferred = []

            def emit_exp_pv(idx):
                h, j = its[idx]
                hp, m = h % 2, h // 2
                r0 = max(0, j - tq0)
                sbk = STB[idx % len(STB)]
                pblk = 16 + idx % 4
                ob = 6 + h % 2
                S.emit("act", lambda e: e.activation(
                    out=wv(pblk)[:, r0 * 128:512], in_=ps[sbk][:, r0 * 128:512],
                    func=AF.Exp, bias=negc[:, j, h:h + 1], scale=1.0),
                    reads=[("ps", sbk), ("negc", j)], writes=[("work", pblk)])
                S.emit("pe", lambda e: e.matmul(ps[ob][0:DH + 1, r0 * 128:512], lhsT=VA[:, j, h, :],
                                                rhs=wv(pblk)[:, r0 * 128:512], start=(j == 0), stop=(j == nj - 1)),
                       reads=[("work", pblk), ("VA", j, h // 8)], writes=[("ps", ob)])
                if j == nj - 1:
                    rdf = scr[5][64:65, 0:T]
                    o1, o2, o3, o4 = (2, 3, 6, 9) if nj >= 8 else (1, 2, 3, 4)
                    bb = h % 2
                    dve_recip = nj >= 12

                    def st1():
                        if dve_recip:
                            S.emit("dve", lambda e: e.reciprocal(out=rdf, in_=ps[ob][64:65, :]),
                                   reads=[("ps", ob)], writes=[("scr", 5)])
                        else:
                            S.emit("act", lambda e: e.activation(out=rdf, in_=ps[ob][64:65, :], func=AF.Ln),
                                   reads=[("ps", ob)], writes=[("scr", 5)])

                    def st2():
                        if not dve_recip:
                            S.emit("act", lambda e: e.activation(out=rdf, in_=rdf, func=AF.Exp, scale=-1.0),
                                   reads=[("scr", 5)], writes=[("scr", 5)])
                        S.emit("dve", lambda e: e.tensor_copy(out=RD[64:65, 0:T], in_=rdf),
                               reads=[("scr", 5)], writes=["RD"])
                        S.emit("dve", lambda e: e.tensor_tensor(out=RD[64:65, T:2 * T], in0=rdf, in1=RD[64:65, 0:T],
                                                                op=ALU.subtract),
                               reads=[("scr", 5), "RD"], writes=["RD"])

                    dst = wv(8 + m)[hp * 64:(hp + 1) * 64, :]

                    def tail1():
                        def fnb(e):
                            e.matmul(ps[bb][0:64, :], lhsT=SELB[64:65, :], rhs=RD[64:65, 0:T], start=True, stop=False)
                            return e.matmul(ps[bb][0:64, :], lhsT=SELB[64:65, :], rhs=RD[64:65, T:2 * T],
                                            start=False, stop=True)
                        S.emit("pe", fnb, reads=["RD", "SELB"], writes=[("ps", bb)])
                        S.emit("dve", lambda e: e.tensor_copy(out=scr[2][0:64, 0:T], in_=ps[bb][0:64, :]),
                               reads=[("ps", bb)], writes=[("scr", 2)])
                        S.emit("dve", lambda e: e.tensor_tensor(out=scr[3][0:64, 0:T], in0=ps[ob][0:64, :],
                                                                in1=scr[2][0:64, 0:T], op=ALU.mult),
                               reads=[("ps", ob), ("scr", 2)], writes=[("scr", 3)])
                        if hp == 1:
                            S.emit("dve", lambda e: e.tensor_copy(out=scr[4][64:128, 0:T], in_=scr[3][0:64, 0:T]),
                                   reads=[("scr", 3)], writes=[("scr", 4)])

                    def tail2():
                        if hp == 0:
                            S.emit("dve", lambda e: e.tensor_tensor(out=dst, in0=scr[3][0:64, 0:T], in1=dst, op=ALU.mult),
                                   reads=[("scr", 3)] + wblk(8 + m), writes=wblk(8 + m))
                        else:
                            S.emit("dve", lambda e: e.tensor_tensor(out=dst, in0=scr[4][64:128, 0:T], in1=dst, op=ALU.mult),
                                   reads=[("scr", 4)] + wblk(8 + m), writes=wblk(8 + m))
                    deferred.append((idx + o1, st1))
                    deferred.append((idx + o2, st2))
                    deferred.append((idx + o3, tail1))
                    deferred.append((idx + o4, tail2))

            for idx in range(min(LAG, len(its))):
                emit_qk(idx)
            for idx in range(len(its)):
                if idx + LAG < len(its):
                    emit_qk(idx + LAG)
                emit_exp_pv(idx)
                while deferred and deferred[0][0] <= idx:
                    deferred.pop(0)[1]()
            while deferred:
                deferred.pop(0)[1]()
            down_proj(lambda m, t: wv(8 + m)[:, t * 128:(t + 1) * 128], lambda m: wblk(8 + m), "wo", fs, after_t)

        STAGES = [("ffn", 0, 0, 1, True), ("conv", None, 2, 3, False), ("ffn", 1, 4, 5, True),
                  ("kv", None, 6, None, None), ("ffn", 2, 7, 8, True), ("attn", None, 9, 10, False),
                  ("ffn", 3, 11, 12, True)][:stop_stage]
        NS = len(STAGES)

        def load_x(ti):
            p = ti % 2
            S.dma("sp", lambda e: e.dma_start(out=Xb[p][:], in_=x_d[ti * T:(ti + 1) * T, :].rearrange("(t p) d -> p t d", p=128)),
                  [], [("X", p, t) for t in range(4)], f"xin{p}")

        load_x(0)
        load_gain(gpre, "gpre", STAGES[0][2], "gpre")
        for t in range(4):
            norm_A(Xb[0], 0, t)
            norm_B(t)
        for ti in range(ntiles_total):
            p = ti % 2
            Xc = Xb[p]
            tis = ti % tps
            fs = use_scratch and ti > 0
            for si, (kind, f, gi_pre, gi_post, half) in enumerate(STAGES):
                last_stage = si == NS - 1
                has_next = (not last_stage) or (ti + 1 < ntiles_total)
                if last_stage:
                    nXc, nxp, ngi = Xb[(ti + 1) % 2], (ti + 1) % 2, STAGES[0][2]
                else:
                    nXc, nxp, ngi = Xc, p, STAGES[si + 1][2]
                if has_next:
                    load_gain(gpre, "gpre", ngi, "gpre")
                if gi_post is not None:
                    load_gain(gpost, "gpost", gi_post, "gpost")
                if si == min(1, NS - 1) and ti + 1 < ntiles_total:
                    load_x(ti + 1)
                inter = has_next and not last_stage

                def after_t(t, Xc=Xc, p=p, half=half, inter=inter, nXc=nXc, nxp=nxp):
                    if inter and t >= 1:
                        epilogue_t(Xc, p, half, t,
                                   hook1=lambda: norm_A1(nXc, nxp, t - 1),
                                   hook2=lambda: norm_A2(nXc, nxp, t - 1))
                    else:
                        epilogue_t(Xc, p, half, t)
                    if inter:
                        if t >= 2:
                            norm_B(t - 2)
                        if t == 3:
                            norm_A(nXc, nxp, 3)
                            norm_B(2)
                            norm_B(3, split=True)

                if kind == "ffn":
                    if last_stage and has_next:
                        ffn_body(f, fs, after_t,
                                 chunk_hook=lambda j: norm_A(nXc, nxp, j - (NJ - 4)) if j >= NJ - 4 else None,
                                 mid_hook=lambda: [norm_B(t) for t in range(4)])
                    else:
                        ffn_body(f, fs, after_t)
                elif kind == "conv":
                    conv_body(tis == 0, fs, after_t)
                elif kind == "kv":
                    if has_next:
                        for t in range(4):
                            norm_A(nXc, nxp, t)
                    kv_body(tis, fs)
                    if has_next:
                        for t in range(4):
                            norm_B(t)
                elif kind == "attn":
                    attn_body(tis, fs, after_t)
            if ti == 0:
                emit_conv(10 ** 6)
            S.dma("sp", lambda e, ti=ti, p=p: e.dma_start(out=y_d[ti * T:(ti + 1) * T, :].rearrange("(t p) d -> p t d", p=128),
                                                         in_=Xb[p][:]),
                  [("X", p, t) for t in range(4)], [], f"yout{p}")
        S.final_wait("sp", ["yout0", "yout1"])

        with nc.Block() as block:
            @block.tensor
            def _(e):
                for op in S.ops["pe"]:
                    op(e)

            @block.scalar
            def _(e):
                for op in S.ops["act"]:
                    op(e)

            @block.vector
            def _(e):
                for op in S.ops["dve"]:
                    op(e)

            @block.gpsimd
            def _(e):
                for op in S.ops["pool"]:
                    op(e)

            @block.sync
            def _(e):
                for op in S.ops["sp"]:
                    op(e)
    return nc


_CACHE = {}


def _weights_map(inp):
    f = lambda a: np.ascontiguousarray(np.asarray(a, dtype=np.float32))
    gains = np.stack([
        inp["ffn1_pre_g"][0], inp["ffn1_post_g"][0], inp["mix_pre_g"][0], inp["mix_post_g"][0],
        inp["ffn2_pre_g"][0], inp["ffn2_post_g"][0], inp["kv_g"],
        inp["ffn1_pre_g"][1], inp["ffn1_post_g"][1], inp["mix_pre_g"][1], inp["mix_post_g"][1],
        inp["ffn2_pre_g"][1], inp["ffn2_post_g"][1]], axis=0)
    kcv = np.asarray(inp["conv_k"][0]).reshape(3, KC, 128).transpose(2, 1, 0).reshape(128, 24)
    return {
        "gains": f(gains), "fb": f(np.asarray(inp["forget_b"]).reshape(1, H)), "kcv": f(kcv),
        "f1wi": f(inp["ffn1_w_in"]), "f1wo": f(inp["ffn1_w_out"]),
        "f2wi": f(inp["ffn2_w_in"]), "f2wo": f(inp["ffn2_w_out"]),
        "cwi": f(inp["conv_w_in"][0]), "cwo": f(inp["conv_w_out"][0]),
        "kvw": f(inp["kv_w"]), "wqg": f(inp["attn_w_qg"][0]), "wo": f(inp["attn_w_o"][0]),
    }


def kernel(**inputs):
    inp = {k: np.asarray(v) for k, v in inputs.items()}
    x = np.ascontiguousarray(inp["x"], dtype=np.float32)
    B, SEQ, _ = x.shape
    ncores = 8
    n_seq = B // ncores
    key = (n_seq, SEQ)
    if key not in _CACHE:
        _CACHE[key] = build_program(n_seq=n_seq, seq=SEQ)
    nc = _CACHE[key]
    wm = _weights_map(inp)
    in_maps = []
    for c in range(ncores):
        m = dict(wm)
        m["x"] = np.ascontiguousarray(x[c * n_seq:(c + 1) * n_seq].reshape(n_seq * SEQ, D))
        in_maps.append(m)
    res = run_bass_kernel_spmd(nc, in_maps, core_ids=list(range(ncores)))
    out = np.concatenate([np.asarray(r["y"]).reshape(n_seq, SEQ, D) for r in res.results], axis=0)
    return out.astype(np.float32)
```

```python
import math
from contextlib import ExitStack

import numpy as np
import concourse.bass as bass
import concourse.mybir as mybir
from concourse.bass_utils import run_bass_kernel_spmd

F32 = mybir.dt.float32
BF16 = mybir.dt.bfloat16
ALU = mybir.AluOpType
AF = mybir.ActivationFunctionType

D = 1024
KC = 8
DFF = 2816
NJ = 22
NG = 11
T = 512
H = 16
DH = 64
EPS = 1e-6
NBIG = 4
NSMALL = 3
NSCR = 6
MASK_NEG = -30000.0
LN_HALF = math.log(0.5)


class Sched:
    ENG = ("pe", "act", "dve", "pool", "sp")

    def __init__(self, nc, stack):
        self.nc = nc
        self.stack = stack
        self.ops = {e: [] for e in self.ENG}
        self.cnt = {e: 0 for e in self.ENG}
        self.sem = {}
        self.dcnt = {}
        self.waited = {e: {} for e in self.ENG}
        self.lastw = {}
        self.readers = {}
        for e in ("pe", "act", "dve", "pool"):
            self.getsem(e)

    def getsem(self, name):
        if name not in self.sem:
            self.sem[name] = self.stack.enter_context(self.nc.semaphore("s_" + name))
        return self.sem[name]

    def _deps(self, eng, reads, writes):
        need = {}

        def add(tok, raw):
            if tok is None:
                return
            s, v = tok
            if s == eng and (eng == "pe" or not raw):
                return
            if v > need.get(s, 0):
                need[s] = v

        for g in reads:
            add(self.lastw.get(g), True)
        for g in writes:
            add(self.lastw.get(g), False)
            for s, v in self.readers.get(g, {}).items():
                add((s, v), False)
        out = []
        for s, v in need.items():
            if self.waited[eng].get(s, 0) < v:
                self.waited[eng][s] = v
                out.append((self.sem[s], v))
        return out

    def _commit(self, tok, reads, writes):
        for g in writes:
            self.lastw[g] = tok
            self.readers[g] = {}
        for g in reads:
            r = self.readers.setdefault(g, {})
            if tok[1] > r.get(tok[0], 0):
                r[tok[0]] = tok[1]

    def emit(self, eng, fn, reads=(), writes=()):
        waits = self._deps(eng, reads, writes)
        self.cnt[eng] += 1
        tok = (eng, self.cnt[eng])
        sem = self.sem[eng]

        def op(e, fn=fn, waits=waits, sem=sem):
            for s, v in waits:
                e.wait_ge(s, v)
            fn(e).then_inc(sem, 1)

        self.ops[eng].append(op)
        self._commit(tok, reads, writes)

    def dma(self, q, fn, reads, writes, semname):
        sem = self.getsem(semname)
        waits = [(s_, v_) for (s_, v_) in self._deps(q, reads, writes) if s_ is not sem]
        self.dcnt[semname] = self.dcnt.get(semname, 0) + 16
        tok = (semname, self.dcnt[semname])

        def op(e, fn=fn, waits=waits, sem=sem):
            for s, v in waits:
                e.wait_ge(s, v)
            fn(e).then_inc(sem, 16)

        self.ops[q].append(op)
        self._commit(tok, reads, writes)

    def final_wait(self, q, semnames):
        items = [(self.sem[n], self.dcnt[n]) for n in semnames if n in self.dcnt]

        def op(e, items=items):
            for s, v in items:
                e.wait_ge(s, v)

        self.ops[q].append(op)


def build_program(n_seq=2, seq=2048, stop_stage=7):
    nc = bass.Bass("TRN2", target_bir_lowering=False)
    ntok = n_seq * seq
    tps = seq // T
    ntiles_total = n_seq * tps
    NQ = seq // 128

    def din(name, shape):
        return nc.dram_tensor(name, list(shape), F32, kind="ExternalInput").ap()

    x_d = din("x", [ntok, D])
    y_d = nc.dram_tensor("y", [ntok, D], F32, kind="ExternalOutput").ap()
    G_d = din("gains", [13, D])
    fb_d = din("fb", [1, H])
    kcv_d = din("kcv", [128, 24])
    f1wi = din("f1wi", [2, D, 2 * DFF])
    f1wo = din("f1wo", [2, DFF, D])
    f2wi = din("f2wi", [2, D, 2 * DFF])
    f2wo = din("f2wo", [2, DFF, D])
    cwi = din("cwi", [D, 3 * D])
    cwo = din("cwo", [D, D])
    kvw = din("kvw", [D, 2 * D + H])
    wqg = din("wqg", [D, 2 * D])
    wo_d = din("wo", [D, D])

    with ExitStack() as st:
        def sb(name, shape, dt):
            return st.enter_context(nc.sbuf_tensor(name, list(shape), dt))

        Xb = [sb(f"Xb{i}", [128, 4, D], F32) for i in range(2)]
        KT = sb("KT", [128, KC, seq], BF16)
        VA = sb("VA", [128, NQ, H, DH + 1], BF16)
        spt = sb("spt", [128, NQ, H], F32)
        negc = sb("negc", [128, NQ, H], F32)
        carry = sb("carry", [128, H], F32)
        xn = [sb(f"xn{i}", [128, D], BF16) for i in range(4)]
        xnT = sb("xnT", [128, KC, T], BF16)
        work = sb("work", [128, 22 * 512], BF16)
        big = [sb(f"big{i}", [128, 4096], BF16) for i in range(NBIG)]
        small = [sb(f"small{i}", [128, 1024], BF16) for i in range(NSMALL)]
        gpre = sb("gpre", [128, D], F32)
        gpost = sb("gpost", [128, D], F32)
        scr = [sb(f"scr{i}", [128, 520], F32) for i in range(NSCR)]
        junk = sb("junk", [128, D], BF16)
        stt = sb("stt", [128, 80], F32)
        SELc = sb("SELc", [128, H], BF16)
        QTz = [sb(f"QTz{i}", [128, T], BF16) for i in range(4)]
        PC = sb("PC", [128, 4, 2, 4, H], BF16)
        R1 = sb("R1", [128, 4, H], F32)
        CQT = sb("CQT", [128, T], BF16)
        RD = sb("RD", [128, 2 * T], BF16)
        SELB = sb("SELB", [128, 64], BF16)
        halo = sb("halo", [128, KC, 2], F32)
        kcv = sb("kcv_sb", [128, 24], F32)
        fb = sb("fb_sb", [128, H], F32)
        zf = sb("zf", [128, H], F32)
        ident = sb("ident", [128, 128], BF16)
        maskM = sb("maskM", [128, 128], BF16)
        Umat = sb("Umat", [128, 128], F32)
        ONES = sb("ONES", [128, 128], F32)
        ps = [st.enter_context(nc.psum_tensor(f"ps{i}", [128, 512], F32)) for i in range(8)]
        psb = [p.bitcast(BF16) for p in ps]

        S = Sched(nc, st)
        state = {"big": 0, "small": 0, "st": 0, "bank": 0}

        def wblk(b0, n=1):
            return [("work", b) for b in range(b0, b0 + n)]

        def wv(b0, n=1):
            return work[:, b0 * 512:(b0 + n) * 512]

        def stcols(n):
            c = state["st"]
            if c + n > 80:
                c = 0
            state["st"] = c + n
            return c

        S.emit("pool", lambda e: e.memset(ident[:], 0.0), writes=["ident"])
        S.emit("pool", lambda e: e.affine_select(out=ident[:], in_=ident[:], pattern=[[-1, 128]],
                                                 compare_op=ALU.not_equal, fill=1.0, base=0,
                                                 channel_multiplier=1),
               reads=["ident"], writes=["ident"])
        S.emit("pool", lambda e: e.memset(maskM[:], 0.0), writes=["maskM"])
        S.emit("pool", lambda e: e.affine_select(out=maskM[:], in_=maskM[:], pattern=[[1, 128]],
                                                 compare_op=ALU.is_ge, fill=MASK_NEG, base=0,
                                                 channel_multiplier=-1),
               reads=["maskM"], writes=["maskM"])
        S.emit("pool", lambda e: e.memset(Umat[:], 1.0), writes=["Umat"])
        S.emit("pool", lambda e: e.affine_select(out=Umat[:], in_=Umat[:], pattern=[[1, 128]],
                                                 compare_op=ALU.is_ge, fill=0.0, base=0,
                                                 channel_multiplier=-1),
               reads=["Umat"], writes=["Umat"])
        S.emit("pool", lambda e: e.memset(ONES[:], 1.0), writes=["ONES"])
        S.emit("pool", lambda e: e.memset(SELc[:], 0.0), writes=["SEL"])
        for off in (0, 16, 32):
            S.emit("pool", lambda e, off=off: e.affine_select(out=SELc[:], in_=SELc[:], pattern=[[-1, H]],
                                                             compare_op=ALU.not_equal, fill=1.0, base=-off,
                                                             channel_multiplier=1),
                   reads=["SEL"], writes=["SEL"])
        for i in range(4):
            S.emit("pool", lambda e, i=i: e.memset(QTz[i][:], 0.0), writes=[("QTz", i)])
        S.emit("pool", lambda e: e.memset(PC[:], 0.0), writes=["PC"])
        S.emit("pool", lambda e: e.memset(RD[:], 0.0), writes=["RD"])
        S.emit("pool", lambda e: e.memset(SELB[:], 0.0), writes=["SELB"])
        S.emit("pool", lambda e: e.memset(SELB[64:65, :], 1.0), writes=["SELB"])
        S.emit("pool", lambda e: e.memset(SELB[96:97, :], 1.0), writes=["SELB"])
        S.emit("pool", lambda e: e.memset(VA[:, :, :, DH:DH + 1], 1.0),
               writes=[("VA", q, hf) for q in range(NQ) for hf in range(2)])
        S.dma("sp", lambda e: e.dma_start(out=kcv[:], in_=kcv_d), [], ["kcv"], "cst0")
        S.dma("sp", lambda e: e.dma_start(out=fb[:], in_=fb_d.broadcast_to([128, H])), [], ["fb"], "cst1")

        units = {}
        ulist = []

        def mk(key, n, parts, ring):
            units[key] = dict(idx=len(ulist), n=n, parts=parts, ring=ring)
            ulist.append(key)

        def kcp(ap):
            return ap.rearrange("(kc p) n -> p kc n", p=128)

        def cpn(ap):
            return ap.rearrange("(c p) n -> p c n", p=128)

        def v3(ap, a):
            return ap.rearrange("p (a b) -> p a b", a=a)

        def sl3(a, b, k):
            return lambda s: v3(s[:, a:b], k)

        ffn_w = [(f1wi[0], f1wo[0]), (f2wi[0], f2wo[0]), (f1wi[1], f1wo[1]), (f2wi[1], f2wo[1])]

        def mk_ffn(f):
            wi, wo = ffn_w[f]
            for g in range(NG):
                mk(("wi", f, g), 4096, [(sl3(0, 2048, KC), kcp(wi[:, g * 256:(g + 1) * 256])),
                                        (sl3(2048, 4096, KC), kcp(wi[:, DFF + g * 256:DFF + (g + 1) * 256]))], "big")
                mk(("woA", f, g), 1024, [(sl3(0, 1024, 2), cpn(wo[g * 256:(g + 1) * 256, 0:512]))], "small")
            for gb in range(3):
                nch = min(8, NJ - 8 * gb)
                mk(("woB", f, gb), nch * 512, [(sl3(0, nch * 512, nch), cpn(wo[gb * 1024:gb * 1024 + nch * 128, 512:1024]))], "big")

        mk_ffn(0)
        for m in range(KC):
            mk(("cwi", m), 3072, [(sl3(i * 1024, (i + 1) * 1024, KC), kcp(cwi[:, i * D + m * 128:i * D + (m + 1) * 128]))
                                  for i in range(3)], "big")
        for hh in range(2):
            mk(("cwo", hh), 4096, [(sl3(0, 4096, 4), cpn(cwo[hh * 512:(hh + 1) * 512, :]))], "big")
        mk_ffn(1)
        for mg in range(2):
            mk(("kvk", mg), 4096, [(sl3(0, 4096, KC), kcp(kvw[:, mg * 512:(mg + 1) * 512]))], "big")
        for hf in range(2):
            mk(("kvv", hf), 4096, [(sl3(0, 4096, KC), kcp(kvw[:, D + hf * 512:D + (hf + 1) * 512]))], "big")
        mk(("kvf",), 128, [(sl3(0, 128, KC), kcp(kvw[:, 2 * D:2 * D + H]))], "small")
        mk_ffn(2)
        for mg in range(2):
            mk(("wq", mg), 4096, [(sl3(0, 4096, KC), kcp(wqg[:, mg * 512:(mg + 1) * 512]))], "big")
        for hf in range(2):
            mk(("wg", hf), 4096, [(sl3(0, 4096, KC), kcp(wqg[:, D + hf * 512:D + (hf + 1) * 512]))], "big")
        for hh in range(2):
            mk(("wo", hh), 4096, [(sl3(0, 4096, 4), cpn(wo_d[hh * 512:(hh + 1) * 512, :]))], "big")
        mk_ffn(3)

        use_scratch = ntiles_total > 1
        if use_scratch:
            sc = nc.dram_tensor("wscratch", [len(ulist), 128, 4096], BF16).ap()
        cv = {"list": [], "pos": 0}
        written_back = set()

        def emit_conv(n):
            if not use_scratch:
                return
            while n > 0 and cv["pos"] < len(cv["list"]):
                key, i = cv["list"][cv["pos"]]
                cv["pos"] += 1
                n -= 1
                u = units[key]
                dfn, src = u["parts"][i]
                dst = dfn(sc[u["idx"]])
                S.dma("pool", lambda e, dst=dst, src=src: e.dma_start(out=dst, in_=src), [], ["cvall"], "cv")

        def load_gain(dst, gran, idx, semname):
            S.dma("sp", lambda e: e.dma_start(out=dst[:], in_=G_d[idx:idx + 1, :].broadcast_to([128, D])),
                  [], [gran], semname)

        def load_unit(key, from_scratch):
            u = units[key]
            if u["ring"] == "big":
                k = state["big"] % NBIG
                state["big"] += 1
                slot, gran, semname = big[k], ("big", k), f"big{k}"
            else:
                k = state["small"] % NSMALL
                state["small"] += 1
                slot, gran, semname = small[k], ("small", k), f"small{k}"
            if from_scratch:
                n = u["n"]
                src = sc[u["idx"], :, 0:n]
                S.dma("pool", lambda e, slot=slot, src=src, n=n: e.dma_start(out=slot[:, 0:n], in_=src),
                      [("sc", key)], [gran], semname)
            else:
                for dfn, src in u["parts"]:
                    S.dma("pool", lambda e, dfn=dfn, src=src, slot=slot: e.dma_start(out=dfn(slot), in_=src),
                          [], [gran], semname)
                if use_scratch and key not in written_back:
                    written_back.add(key)
                    n = u["n"]
                    dst = sc[u["idx"], :, 0:n]
                    S.dma("sp", lambda e, slot=slot, dst=dst, n=n: e.dma_start(out=dst, in_=slot[:, 0:n]),
                          [gran], [("sc", key)], "wb_" + semname)
            return slot, [gran]

        nst = {}

        def norm_A1(Xc, xp, t):
            c = stcols(3)
            nst[t] = c
            S.emit("act", lambda e: e.activation(out=junk[:], in_=Xc[:, t, :], func=AF.Square,
                                                 scale=1.0 / 32.0, accum_out=stt[:, c:c + 1]),
                   reads=[("X", xp, t)], writes=["junk", ("st", c)])
            S.emit("act", lambda e: e.activation(out=stt[:, c + 1:c + 2], in_=stt[:, c:c + 1], func=AF.Ln, bias=EPS),
                   reads=[("st", c)], writes=[("st", c + 1)])
            S.emit("act", lambda e: e.activation(out=stt[:, c + 2:c + 3], in_=stt[:, c + 1:c + 2], func=AF.Exp, scale=-0.5),
                   reads=[("st", c + 1)], writes=[("st", c + 2)])

        def norm_A2(Xc, xp, t):
            c = nst.pop(t)
            S.emit("dve", lambda e: e.scalar_tensor_tensor(out=xn[t][:], in0=Xc[:, t, :], scalar=stt[:, c + 2:c + 3],
                                                           in1=gpre[:], op0=ALU.mult, op1=ALU.mult),
                   reads=[("X", xp, t), ("st", c + 2), "gpre"], writes=[("xn", t)])

        def norm_A(Xc, xp, t):
            norm_A1(Xc, xp, t)
            norm_A2(Xc, xp, t)

        def norm_B(t, split=False):
            def tr(e):
                last = None
                for m in range(KC):
                    last = e.transpose(out=psb[t][:, m * 128:(m + 1) * 128],
                                       in_=xn[t][:, m * 128:(m + 1) * 128], identity=ident[:])
                return last
            S.emit("pe", tr, reads=[("xn", t), "ident"], writes=[("ps", t)])
            if split:
                S.emit("act", lambda e: e.activation(out=xnT[:, 0:4, t * 128:(t + 1) * 128],
                                                     in_=v3(psb[t][:, 0:512], 4), func=AF.Copy),
                       reads=[("ps", t)], writes=[("xnT", t)])
                S.emit("dve", lambda e: e.tensor_copy(out=xnT[:, 4:8, t * 128:(t + 1) * 128],
                                                      in_=v3(psb[t][:, 512:1024], 4)),
                       reads=[("ps", t)], writes=[("xnT", t)])
            else:
                S.emit("act", lambda e: e.activation(out=xnT[:, :, t * 128:(t + 1) * 128],
                                                     in_=v3(psb[t][:, :], KC), func=AF.Copy),
                       reads=[("ps", t)], writes=[("xnT", t)])

        XNT = [("xnT", t) for t in range(4)]

        sqA = {}

        def epi_sqA(t):
            c = stcols(5)
            sqA[t] = c
            S.emit("act", lambda e: e.activation(out=junk[:, 0:512], in_=ps[4 + t][:], func=AF.Square,
                                                 scale=1.0 / 32.0, accum_out=stt[:, c:c + 1]),
                   reads=[("ps", 4 + t)], writes=["junk", ("st", c)])

        def epilogue_t(Xc, xp, half, t, hook1=None, hook2=None):
            if t not in sqA:
                epi_sqA(t)
            c = sqA.pop(t)
            S.emit("act", lambda e: e.activation(out=junk[:, 512:1024], in_=ps[t][:], func=AF.Square,
                                                 scale=1.0 / 32.0, accum_out=stt[:, c + 1:c + 2]),
                   reads=[("ps", t)], writes=["junk", ("st", c + 1)])
            if hook1 is not None:
                hook1()
            S.emit("dve", lambda e: e.tensor_tensor(out=stt[:, c + 2:c + 3], in0=stt[:, c:c + 1],
                                                    in1=stt[:, c + 1:c + 2], op=ALU.add),
                   reads=[("st", c), ("st", c + 1)], writes=[("st", c + 2)])

            S.emit("act", lambda e: e.activation(out=stt[:, c + 3:c + 4], in_=stt[:, c + 2:c + 3], func=AF.Ln, bias=EPS),
                   reads=[("st", c + 2)], writes=[("st", c + 3)])
            S.emit("act", lambda e: e.activation(out=stt[:, c + 4:c + 5], in_=stt[:, c + 3:c + 4], func=AF.Exp,
                                                 scale=-0.5, bias=(LN_HALF if half else 0.0)),
                   reads=[("st", c + 3)], writes=[("st", c + 4)])
            for hf, bank in enumerate((4 + t, t)):
                S.emit("dve", lambda e, hf=hf, bank=bank: e.scalar_tensor_tensor(
                    out=scr[4 + hf][:, 0:512], in0=ps[bank][:], scalar=stt[:, c + 4:c + 5],
                    in1=gpost[:, hf * 512:(hf + 1) * 512], op0=ALU.mult, op1=ALU.mult),
                    reads=[("ps", bank), ("st", c + 4), "gpost"], writes=[("scr", 4 + hf)])
            for hf in range(2):
                S.emit("dve", lambda e, hf=hf: e.tensor_tensor(
                    out=Xc[:, t, hf * 512:(hf + 1) * 512], in0=scr[4 + hf][:, 0:512],
                    in1=Xc[:, t, hf * 512:(hf + 1) * 512], op=ALU.add),
                    reads=[("scr", 4 + hf), ("X", xp, t)], writes=[("X", xp, t)])
            if hook2 is not None:
                hook2()

        def sigmoid_chain(src_bank_ap, src_gran, dst_ap, dst_gran, si):
            tmp = scr[si][:, 0:512]
            S.emit("act", lambda e: e.activation(out=tmp, in_=src_bank_ap, func=AF.Exp, scale=-1.0),
                   reads=[src_gran], writes=[("scr", si)])
            S.emit("act", lambda e: e.activation(out=tmp, in_=tmp, func=AF.Ln, bias=1.0),
                   reads=[("scr", si)], writes=[("scr", si)])
            S.emit("act", lambda e: e.activation(out=dst_ap, in_=tmp, func=AF.Exp, scale=-1.0),
                   reads=[("scr", si)], writes=dst_gran)

        def ffn_body(f, fs, after_t, chunk_hook=None, mid_hook=None):
            pend = None

            def emit_DA(j, sslot, sgr, jj):
                def fn(e):
                    last = None
                    for t in range(4):
                        last = e.matmul(ps[4 + t][:], lhsT=wv(j)[:, t * 128:(t + 1) * 128],
                                        rhs=sslot[:, jj * 512:(jj + 1) * 512],
                                        start=(j == 0), stop=(j == NJ - 1))
                    return last
                S.emit("pe", fn, reads=wblk(j) + sgr, writes=[("ps", 4 + t) for t in range(4)])

            for g in range(NG):
                slot, bgr = load_unit(("wi", f, g), fs)
                sslot, sgr = load_unit(("woA", f, g), fs)
                for jj in range(2):
                    j = 2 * g + jj
                    pg, pu = (0, 1) if j % 2 == 0 else (2, 3)
                    for which, bank in ((0, pg), (1, pu)):
                        def fn(e, which=which, bank=bank, jj=jj, slot=slot):
                            last = None
                            for kc in range(KC):
                                base = which * 2048 + kc * 256 + jj * 128
                                last = e.matmul(ps[bank][:], lhsT=slot[:, base:base + 128], rhs=xnT[:, kc, :],
                                                start=(kc == 0), stop=(kc == KC - 1))
                            return last
                        S.emit("pe", fn, reads=bgr + XNT, writes=[("ps", bank)])
                    if pend is not None:
                        emit_DA(*pend)
                    si = j % 2
                    sigmoid_chain(ps[pg][:], ("ps", pg), scr[si][:, 0:512], [("scr", si)], si)
                    S.emit("dve", lambda e, si=si, pg=pg: e.tensor_tensor(out=scr[2 + si][:, 0:512], in0=ps[pg][:],
                                                                          in1=scr[si][:, 0:512], op=ALU.mult),
                           reads=[("ps", pg), ("scr", si)], writes=[("scr", 2 + si)])
                    S.emit("dve", lambda e, si=si, pu=pu, j=j: e.tensor_tensor(out=wv(j), in0=ps[pu][:],
                                                                              in1=scr[2 + si][:, 0:512], op=ALU.mult),
                           reads=[("ps", pu), ("scr", 2 + si)], writes=wblk(j))
                    pend = (j, sslot, sgr, jj)
                    if chunk_hook is not None:
                        chunk_hook(j)
            bslots = None

            def emit_B(t, gb):
                nch = min(8, NJ - 8 * gb)
                slot, bgr = bslots[gb]

                def fn(e, t=t, gb=gb, nch=nch, slot=slot):
                    last = None
                    for c_ in range(nch):
                        j = 8 * gb + c_
                        last = e.matmul(ps[t][:], lhsT=wv(j)[:, t * 128:(t + 1) * 128],
                                        rhs=slot[:, c_ * 512:(c_ + 1) * 512],
                                        start=(j == 0), stop=(j == NJ - 1))
                    return last
                S.emit("pe", fn, reads=wblk(8 * gb, nch) + bgr, writes=[("ps", t)])

            early = []
            if mid_hook is None:
                bslots = [load_unit(("woB", f, gb), fs) for gb in range(3)]
                for gb in range(2):
                    emit_B(0, gb)
                    early.append((0, gb))
            emit_DA(*pend)
            if mid_hook is not None:
                mid_hook()
            for t in range(4):
                epi_sqA(t)
            if bslots is None:
                bslots = [load_unit(("woB", f, gb), fs) for gb in range(3)]
            for t in range(4):
                for gb in range(3):
                    if (t, gb) not in early:
                        emit_B(t, gb)
                after_t(t)

        def down_proj(lhs_fn, lhs_gran, key, fs, after_t):
            slots = [load_unit((key, hh), fs) for hh in range(2)]
            sgr = slots[0][1] + slots[1][1]

            def part(t, hf, bank, ms):
                ms = list(ms)

                def fn(e):
                    last = None
                    for m in ms:
                        slot = slots[m // 4][0]
                        base = (m % 4) * 1024 + hf * 512
                        last = e.matmul(ps[bank][:], lhsT=lhs_fn(m, t), rhs=slot[:, base:base + 512],
                                        start=(m == 0), stop=(m == KC - 1))
                    return last
                S.emit("pe", fn, reads=[g for m in ms for g in lhs_gran(m)] + sgr, writes=[("ps", bank)])

            for t in (0,):
                for hf, bank in enumerate((4 + t, t)):
                    part(t, hf, bank, range(KC - 1))
            for t in (0,):
                for hf, bank in enumerate((4 + t, t)):
                    part(t, hf, bank, [KC - 1])
                after_t(t)
            for t in (1, 2, 3):
                for hf, bank in enumerate((4 + t, t)):
                    part(t, hf, bank, range(KC))
                after_t(t)

        def conv_body(first_tile, fs, after_t):
            if first_tile:
                S.emit("dve", lambda e: e.memset(halo[:], 0.0), writes=[("halo", m) for m in range(KC)])
            for m in range(KC):
                slot, bgr = load_unit(("cwi", m), fs)
                banks = (0, 1, 2) if m % 2 == 0 else (3, 4, 5)
                for i in range(3):
                    def fn(e, i=i, slot=slot, bank=banks[i]):
                        last = None
                        for kc in range(KC):
                            base = i * 1024 + kc * 128
                            last = e.matmul(ps[bank][:], lhsT=slot[:, base:base + 128], rhs=xnT[:, kc, :],
                                            start=(kc == 0), stop=(kc == KC - 1))
                        return last
                    S.emit("pe", fn, reads=bgr + XNT, writes=[("ps", banks[i])])
                pb_, pc_, ph_ = banks
                ci = m % 2
                ui = 2 + m % 2
                yi = 4 + m % 2
                S.emit("act", lambda e, ci=ci, pc_=pc_: e.activation(out=scr[ci][:, 0:512], in_=ps[pc_][:], func=AF.Copy),
                       reads=[("ps", pc_)], writes=[("scr", ci)])
                S.emit("act", lambda e, ui=ui, m=m: e.activation(out=scr[ui][:, 0:2], in_=halo[:, m, :], func=AF.Copy),
                       reads=[("halo", m)], writes=[("scr", ui)])
                S.emit("dve", lambda e, ui=ui, ci=ci, ph_=ph_: e.tensor_tensor(out=scr[ui][:, 2:514], in0=ps[ph_][:],
                                                                                in1=scr[ci][:, 0:512], op=ALU.mult),
                       reads=[("ps", ph_), ("scr", ci), ("scr", ui)], writes=[("scr", ui)])
                S.emit("act", lambda e, ui=ui, m=m: e.activation(out=halo[:, m, :], in_=scr[ui][:, 512:514], func=AF.Copy),
                       reads=[("scr", ui)], writes=[("halo", m)])
                S.emit("dve", lambda e, ui=ui, yi=yi, m=m: e.tensor_scalar(
                    out=scr[yi][:, 0:512], in0=scr[ui][:, 2:514], scalar1=kcv[:, m * 3 + 2:m * 3 + 3],
                    scalar2=None, op0=ALU.mult),
                    reads=[("scr", ui), "kcv"], writes=[("scr", yi)])
                for w_, off in ((1, 1), (0, 0)):
                    S.emit("dve", lambda e, ui=ui, yi=yi, m=m, w_=w_, off=off: e.scalar_tensor_tensor(
                        out=scr[yi][:, 0:512], in0=scr[ui][:, off:off + 512], scalar=kcv[:, m * 3 + w_:m * 3 + w_ + 1],
                        in1=scr[yi][:, 0:512], op0=ALU.mult, op1=ALU.add),
                        reads=[("scr", ui), ("scr", yi), "kcv"], writes=[("scr", yi)])
                S.emit("dve", lambda e, yi=yi, m=m, pb_=pb_: e.tensor_tensor(out=wv(m), in0=ps[pb_][:],
                                                                             in1=scr[yi][:, 0:512], op=ALU.mult),
                       reads=[("ps", pb_), ("scr", yi)], writes=wblk(m))
            down_proj(lambda m, t: wv(m)[:, t * 128:(t + 1) * 128], lambda m: wblk(m), "cwo", fs, after_t)

        def kv_body(tile_in_seq, fs):
            tq0 = tile_in_seq * 4
            if tile_in_seq == 0:
                S.emit("dve", lambda e: e.memset(carry[:], 0.0), writes=["carry"])
            fslot, fgr = load_unit(("kvf",), fs)
            for t in range(4):
                bank = 4 + t

                def fn(e, t=t, bank=bank):
                    last = None
                    for kc in range(KC):
                        last = e.matmul(ps[bank][:, 0:H], lhsT=xnT[:, kc, t * 128:(t + 1) * 128],
                                        rhs=fslot[:, kc * H:(kc + 1) * H],
                                        start=(kc == 0), stop=(kc == KC - 1))
                    return last
                S.emit("pe", fn, reads=fgr + [("xnT", t)], writes=[("ps", bank)])
                S.emit("dve", lambda e, bank=bank: e.tensor_tensor(out=zf[:], in0=ps[bank][:, 0:H], in1=fb[:], op=ALU.add),
                       reads=[("ps", bank), "fb"], writes=["zf"])
                S.emit("act", lambda e: e.activation(out=zf[:], in_=zf[:], func=AF.Exp, scale=-1.0),
                       reads=["zf"], writes=["zf"])
                S.emit("act", lambda e, t=t: e.activation(out=spt[:, tq0 + t, :], in_=zf[:], func=AF.Ln, bias=1.0),
                       reads=["zf"], writes=[("spt", tq0 + t)])
            for mg in range(2):
                slot, bgr = load_unit(("kvk", mg), fs)
                for mm in range(4):
                    m = mg * 4 + mm
                    bank = m % 4

                    def fn(e, mm=mm, slot=slot, bank=bank):
                        last = None
                        for kc in range(KC):
                            base = kc * 512 + mm * 128
                            last = e.matmul(ps[bank][:], lhsT=slot[:, base:base + 128], rhs=xnT[:, kc, :],
                                            start=(kc == 0), stop=(kc == KC - 1))
                        return last
                    S.emit("pe", fn, reads=bgr + XNT, writes=[("ps", bank)])
                    if m % 2 == 0:
                        S.emit("act", lambda e, m=m, bank=bank: e.activation(
                            out=KT[:, m, tile_in_seq * T:(tile_in_seq + 1) * T], in_=ps[bank][:], func=AF.Copy),
                            reads=[("ps", bank)], writes=[("KT", tile_in_seq, m)])
                    else:
                        S.emit("dve", lambda e, m=m, bank=bank: e.tensor_copy(
                            out=KT[:, m, tile_in_seq * T:(tile_in_seq + 1) * T], in_=ps[bank][:]),
                            reads=[("ps", bank)], writes=[("KT", tile_in_seq, m)])
            sgr = [("spt", tq0 + t) for t in range(4)]
            S.emit("pe", lambda e: e.matmul(ps[4][:, 0:4 * H], lhsT=Umat[:], rhs=spt[:, tq0:tq0 + 4, :],
                                            start=True, stop=True),
                   reads=sgr + ["Umat"], writes=[("ps", 4)])
            S.emit("pe", lambda e: e.matmul(ps[5][:, 0:4 * H], lhsT=ONES[:], rhs=spt[:, tq0:tq0 + 4, :],
                                            start=True, stop=True),
                   reads=sgr + ["ONES"], writes=[("ps", 5)])
            for t in range(4):
                S.emit("dve", lambda e, t=t: e.tensor_tensor(out=negc[:, tq0 + t, :], in0=ps[4][:, t * H:(t + 1) * H],
                                                             in1=carry[:], op=ALU.add),
                       reads=[("ps", 4), "carry"], writes=[("negc", tq0 + t)])
                S.emit("dve", lambda e, t=t: e.tensor_tensor(out=carry[:], in0=ps[5][:, t * H:(t + 1) * H],
                                                             in1=carry[:], op=ALU.add),
                       reads=[("ps", 5), "carry"], writes=["carry"])
            vslots = [load_unit(("kvv", hf), fs) for hf in range(2)]
            for t in range(4):
                for hf in range(2):
                    bank = 4 + (t * 2 + hf) % 4
                    slot, bgr = vslots[hf]

                    def fn(e, t=t, slot=slot, bank=bank):
                        last = None
                        for kc in range(KC):
                            last = e.matmul(ps[bank][:], lhsT=xnT[:, kc, t * 128:(t + 1) * 128],
                                            rhs=slot[:, kc * 512:(kc + 1) * 512],
                                            start=(kc == 0), stop=(kc == KC - 1))
                        return last
                    S.emit("pe", fn, reads=bgr + [("xnT", t)], writes=[("ps", bank)])
                    dst = VA[:, tq0 + t, hf * 8:(hf + 1) * 8, 0:DH]
                    if (t + hf) % 2 == 0:
                        S.emit("act", lambda e, dst=dst, bank=bank: e.activation(out=dst, in_=v3(ps[bank][:], 8), func=AF.Copy),
                               reads=[("ps", bank)], writes=[("VA", tq0 + t, hf)])
                    else:
                        S.emit("dve", lambda e, dst=dst, bank=bank: e.tensor_copy(out=dst, in_=v3(ps[bank][:], 8)),
                               reads=[("ps", bank)], writes=[("VA", tq0 + t, hf)])

        def attn_body(tile_in_seq, fs, after_t):
            tq0 = tile_in_seq * 4
            nj = tq0 + 4
            nsl = negc[:, tq0:tq0 + 4, :]
            ngr = [("negc", tq0 + r) for r in range(4)]
            S.emit("dve", lambda e: e.tensor_scalar(out=PC[:, :, 0, 0, :], in0=nsl, scalar1=-1.0, scalar2=None, op0=ALU.mult),
                   reads=ngr, writes=["PC"])
            S.emit("dve", lambda e: e.scalar_tensor_tensor(out=R1[:], in0=nsl, scalar=-1.0, in1=PC[:, :, 0, 0, :],
                                                           op0=ALU.mult, op1=ALU.subtract),
                   reads=ngr + ["PC"], writes=["R1"])
            S.emit("dve", lambda e: e.tensor_copy(out=PC[:, :, 0, 1, :], in_=R1[:]), reads=["R1"], writes=["PC"])
            S.emit("dve", lambda e: e.tensor_tensor(out=R1[:], in0=R1[:], in1=PC[:, :, 0, 1, :], op=ALU.subtract),
                   reads=["R1", "PC"], writes=["R1"])
            S.emit("dve", lambda e: e.tensor_copy(out=PC[:, :, 0, 2, :], in_=R1[:]), reads=["R1"], writes=["PC"])
            S.emit("dve", lambda e: e.tensor_copy(out=PC[:, :, 1, :, :], in_=PC[:, :, 0, :, :]), reads=["PC"], writes=["PC"])

            for mg in range(2):
                slot, bgr = load_unit(("wq", mg), fs)
                for mm in range(4):
                    m = mg * 4 + mm
                    bank = m % 4

                    def fn(e, mm=mm, slot=slot, bank=bank):
                        last = None
                        for kc in range(KC):
                            base = kc * 512 + mm * 128
                            last = e.matmul(ps[bank][:], lhsT=slot[:, base:base + 128], rhs=xnT[:, kc, :],
                                            start=(kc == 0), stop=(kc == KC - 1))
                        return last
                    S.emit("pe", fn, reads=bgr + XNT, writes=[("ps", bank)])
                    S.emit("act", lambda e, m=m, bank=bank: e.activation(out=wv(m), in_=ps[bank][:], func=AF.Copy,
                                                                         scale=0.125),
                           reads=[("ps", bank)], writes=wblk(m))
            def trc(e):
                last = None
                for r in range(4):
                    last = e.transpose(out=psb[4][:, r * 128:(r + 1) * 128],
                                       in_=PC[:, r, :, :, :].rearrange("p d c h -> p (d c h)"), identity=ident[:])
                return last
            S.emit("pe", trc, reads=["PC", "ident"], writes=[("ps", 4)])
            S.emit("act", lambda e: e.activation(out=CQT[:], in_=psb[4][:, 0:T], func=AF.Copy),
                   reads=[("ps", 4)], writes=["CQT"])

            for hf in range(2):
                slot, bgr = load_unit(("wg", hf), fs)
                for mm in range(4):
                    m = hf * 4 + mm
                    bank = m % 4

                    def fn(e, mm=mm, slot=slot, bank=bank):
                        last = None
                        for kc in range(KC):
                            base = kc * 512 + mm * 128
                            last = e.matmul(ps[bank][:], lhsT=slot[:, base:base + 128], rhs=xnT[:, kc, :],
                                            start=(kc == 0), stop=(kc == KC - 1))
                        return last
                    S.emit("pe", fn, reads=bgr + XNT, writes=[("ps", bank)])
                    sigmoid_chain(ps[bank][:], ("ps", bank), wv(8 + m), wblk(8 + m), m % 2)
            its = [(h, j) for h in range(H) for j in range(nj)]
            STB = (4, 5, 3, 2)
            LAG = 3

            def emit_qk(idx):
                h, j = its[idx]
                hp, m = h % 2, h // 2
                r0 = max(0, j - tq0)
                diag = j >= tq0
                sbk = STB[idx % len(STB)]
                zi = 2 * hp + (m % 2)
                if j == 0:
                    S.emit("dve", lambda e: e.tensor_copy(out=QTz[zi][hp * 64:(hp + 1) * 64, :],
                                                          in_=wv(m)[hp * 64:(hp + 1) * 64, :]),
                           reads=wblk(m), writes=[("QTz", zi)])

                def fn(e):
                    e.matmul(ps[sbk][:, r0 * 128:512], lhsT=KT[:, m, j * 128:(j + 1) * 128],
                             rhs=QTz[zi][:, r0 * 128:512], start=True, stop=False)
                    last = e.matmul(ps[sbk][:, r0 * 128:512], lhsT=SELc[:, h:h + 1].to_broadcast([128, 128]),
                                    rhs=CQT[:, r0 * 128:512], start=False, stop=(not diag))
                    if diag:
                        last = e.matmul(ps[sbk][:, r0 * 128:(r0 + 1) * 128], lhsT=ident[:], rhs=maskM[:],
                                        start=False, stop=True)
                    return last
                S.emit("pe", fn, reads=[("KT", j // 4, m), ("QTz", zi), "ident", "maskM", "SEL", "CQT"],
                       writes=[("ps", sbk)])

            deferred = []

            def emit_exp_pv(idx):
                h, j = its[idx]
                hp, m = h % 2, h // 2
                r0 = max(0, j - tq0)
                sbk = STB[idx % len(STB)]
                pblk = 16 + idx % 4
                ob = 6 + h % 2
                S.emit("act", lambda e: e.activation(
                    out=wv(pblk)[:, r0 * 128:512], in_=ps[sbk][:, r0 * 128:512],
                    func=AF.Exp, bias=negc[:, j, h:h + 1], scale=1.0),
                    reads=[("ps", sbk), ("negc", j)], writes=[("work", pblk)])
                S.emit("pe", lambda e: e.matmul(ps[ob][0:DH + 1, r0 * 128:512], lhsT=VA[:, j, h, :],
                                                rhs=wv(pblk)[:, r0 * 128:512], start=(j == 0), stop=(j == nj - 1)),
                       reads=[("work", pblk), ("VA", j, h // 8)], writes=[("ps", ob)])
                if j == nj - 1:
                    rdf = scr[5][64:65, 0:T]
                    o1, o2, o3, o4 = (2, 3, 6, 9) if nj >= 8 else (1, 2, 3, 4)
                    bb = h % 2
                    dve_recip = nj >= 12

                    def st1():
                        if dve_recip:
                            S.emit("dve", lambda e: e.reciprocal(out=rdf, in_=ps[ob][64:65, :]),
                                   reads=[("ps", ob)], writes=[("scr", 5)])
                        else:
                            S.emit("act", lambda e: e.activation(out=rdf, in_=ps[ob][64:65, :], func=AF.Ln),
                                   reads=[("ps", ob)], writes=[("scr", 5)])

                    def st2():
                        if not dve_recip:
                            S.emit("act", lambda e: e.activation(out=rdf, in_=rdf, func=AF.Exp, scale=-1.0),
                                   reads=[("scr", 5)], writes=[("scr", 5)])
                        S.emit("dve", lambda e: e.tensor_copy(out=RD[64:65, 0:T], in_=rdf),
                               reads=[("scr", 5)], writes=["RD"])
                        S.emit("dve", lambda e: e.tensor_tensor(out=RD[64:65, T:2 * T], in0=rdf, in1=RD[64:65, 0:T],
                                                                op=ALU.subtract),
                               reads=[("scr", 5), "RD"], writes=["RD"])

                    dst = wv(8 + m)[hp * 64:(hp + 1) * 64, :]

                    def tail1():
                        def fnb(e):
                            e.matmul(ps[bb][0:64, :], lhsT=SELB[64:65, :], rhs=RD[64:65, 0:T], start=True, stop=False)
                            return e.matmul(ps[bb][0:64, :], lhsT=SELB[64:65, :], rhs=RD[64:65, T:2 * T],
                                            start=False, stop=True)
                        S.emit("pe", fnb, reads=["RD", "SELB"], writes=[("ps", bb)])
                        S.emit("dve", lambda e: e.tensor_copy(out=scr[2][0:64, 0:T], in_=ps[bb][0:64, :]),
                               reads=[("ps", bb)], writes=[("scr", 2)])
                        S.emit("dve", lambda e: e.tensor_tensor(out=scr[3][0:64, 0:T], in0=ps[ob][0:64, :],
                                                                in1=scr[2][0:64, 0:T], op=ALU.mult),
                               reads=[("ps", ob), ("scr", 2)], writes=[("scr", 3)])
                        if hp == 1:
                            S.emit("dve", lambda e: e.tensor_copy(out=scr[4][64:128, 0:T], in_=scr[3][0:64, 0:T]),
                                   reads=[("scr", 3)], writes=[("scr", 4)])

                    def tail2():
                        if hp == 0:
                            S.emit("dve", lambda e: e.tensor_tensor(out=dst, in0=scr[3][0:64, 0:T], in1=dst, op=ALU.mult),
                                   reads=[("scr", 3)] + wblk(8 + m), writes=wblk(8 + m))
                        else:
                            S.emit("dve", lambda e: e.tensor_tensor(out=dst, in0=scr[4][64:128, 0:T], in1=dst, op=ALU.mult),
                                   reads=[("scr", 4)] + wblk(8 + m), writes=wblk(8 + m))
                    deferred.append((idx + o1, st1))
                    deferred.append((idx + o2, st2))
                    deferred.append((idx + o3, tail1))
                    deferred.append((idx + o4, tail2))

            for idx in range(min(LAG, len(its))):
                emit_qk(idx)
            for idx in range(len(its)):
                if idx + LAG < len(its):
                    emit_qk(idx + LAG)
                emit_exp_pv(idx)
                while deferred and deferred[0][0] <= idx:
                    deferred.pop(0)[1]()
            while deferred:
                deferred.pop(0)[1]()
            down_proj(lambda m, t: wv(8 + m)[:, t * 128:(t + 1) * 128], lambda m: wblk(8 + m), "wo", fs, after_t)

        STAGES = [("ffn", 0, 0, 1, True), ("conv", None, 2, 3, False), ("ffn", 1, 4, 5, True),
                  ("kv", None, 6, None, None), ("ffn", 2, 7, 8, True), ("attn", None, 9, 10, False),
                  ("ffn", 3, 11, 12, True)][:stop_stage]
        NS = len(STAGES)

        def load_x(ti):
            p = ti % 2
            S.dma("sp", lambda e: e.dma_start(out=Xb[p][:], in_=x_d[ti * T:(ti + 1) * T, :].rearrange("(t p) d -> p t d", p=128)),
                  [], [("X", p, t) for t in range(4)], f"xin{p}")

        load_x(0)
        load_gain(gpre, "gpre", STAGES[0][2], "gpre")
        for t in range(4):
            norm_A(Xb[0], 0, t)
            norm_B(t)
        for ti in range(ntiles_total):
            p = ti % 2
            Xc = Xb[p]
            tis = ti % tps
            fs = use_scratch and ti > 0
            for si, (kind, f, gi_pre, gi_post, half) in enumerate(STAGES):
                last_stage = si == NS - 1
                has_next = (not last_stage) or (ti + 1 < ntiles_total)
                if last_stage:
                    nXc, nxp, ngi = Xb[(ti + 1) % 2], (ti + 1) % 2, STAGES[0][2]
                else:
                    nXc, nxp, ngi = Xc, p, STAGES[si + 1][2]
                if has_next:
                    load_gain(gpre, "gpre", ngi, "gpre")
                if gi_post is not None:
                    load_gain(gpost, "gpost", gi_post, "gpost")
                if si == min(1, NS - 1) and ti + 1 < ntiles_total:
                    load_x(ti + 1)
                inter = has_next and not last_stage

                def after_t(t, Xc=Xc, p=p, half=half, inter=inter, nXc=nXc, nxp=nxp):
                    if inter and t >= 1:
                        epilogue_t(Xc, p, half, t,
                                   hook1=lambda: norm_A1(nXc, nxp, t - 1),
                                   hook2=lambda: norm_A2(nXc, nxp, t - 1))
                    else:
                        epilogue_t(Xc, p, half, t)
                    if inter:
                        if t >= 2:
                            norm_B(t - 2)
                        if t == 3:
                            norm_A(nXc, nxp, 3)
                            norm_B(2)
                            norm_B(3, split=True)

                if kind == "ffn":
                    if last_stage and has_next:
                        ffn_body(f, fs, after_t,
                                 chunk_hook=lambda j: norm_A(nXc, nxp, j - (NJ - 4)) if j >= NJ - 4 else None,
                                 mid_hook=lambda: [norm_B(t) for t in range(4)])
                    else:
                        ffn_body(f, fs, after_t)
                elif kind == "conv":
                    conv_body(tis == 0, fs, after_t)
                elif kind == "kv":
                    if has_next:
                        for t in range(4):
                            norm_A(nXc, nxp, t)
                    kv_body(tis, fs)
                    if has_next:
                        for t in range(4):
                            norm_B(t)
                elif kind == "attn":
                    attn_body(tis, fs, after_t)
            if ti == 0:
                emit_conv(10 ** 6)
            S.dma("sp", lambda e, ti=ti, p=p: e.dma_start(out=y_d[ti * T:(ti + 1) * T, :].rearrange("(t p) d -> p t d", p=128),
                                                         in_=Xb[p][:]),
                  [("X", p, t) for t in range(4)], [], f"yout{p}")
        S.final_wait("sp", ["yout0", "yout1"])

        with nc.Block() as block:
            @block.tensor
            def _(e):
                for op in S.ops["pe"]:
                    op(e)

            @block.scalar
            def _(e):
                for op in S.ops["act"]:
                    op(e)

            @block.vector
            def _(e):
                for op in S.ops["dve"]:
                    op(e)

            @block.gpsimd
            def _(e):
                for op in S.ops["pool"]:
                    op(e)

            @block.sync
            def _(e):
                for op in S.ops["sp"]:
                    op(e)
    return nc


_CACHE = {}


def _weights_map(inp):
    f = lambda a: np.ascontiguousarray(np.asarray(a, dtype=np.float32))
    gains = np.stack([
        inp["ffn1_pre_g"][0], inp["ffn1_post_g"][0], inp["mix_pre_g"][0], inp["mix_post_g"][0],
        inp["ffn2_pre_g"][0], inp["ffn2_post_g"][0], inp["kv_g"],
        inp["ffn1_pre_g"][1], inp["ffn1_post_g"][1], inp["mix_pre_g"][1], inp["mix_post_g"][1],
        inp["ffn2_pre_g"][1], inp["ffn2_post_g"][1]], axis=0)
    kcv = np.asarray(inp["conv_k"][0]).reshape(3, KC, 128).transpose(2, 1, 0).reshape(128, 24)
    return {
        "gains": f(gains), "fb": f(np.asarray(inp["forget_b"]).reshape(1, H)), "kcv": f(kcv),
        "f1wi": f(inp["ffn1_w_in"]), "f1wo": f(inp["ffn1_w_out"]),
        "f2wi": f(inp["ffn2_w_in"]), "f2wo": f(inp["ffn2_w_out"]),
        "cwi": f(inp["conv_w_in"][0]), "cwo": f(inp["conv_w_out"][0]),
        "kvw": f(inp["kv_w"]), "wqg": f(inp["attn_w_qg"][0]), "wo": f(inp["attn_w_o"][0]),
    }


def kernel(**inputs):
    inp = {k: np.asarray(v) for k, v in inputs.items()}
    x = np.ascontiguousarray(inp["x"], dtype=np.float32)
    B, SEQ, _ = x.shape
    ncores = 8
    n_seq = B // ncores
    key = (n_seq, SEQ)
    if key not in _CACHE:
        _CACHE[key] = build_program(n_seq=n_seq, seq=SEQ)
    nc = _CACHE[key]
    wm = _weights_map(inp)
    in_maps = []
    for c in range(ncores):
        m = dict(wm)
        m["x"] = np.ascontiguousarray(x[c * n_seq:(c + 1) * n_seq].reshape(n_seq * SEQ, D))
        in_maps.append(m)
    res = run_bass_kernel_spmd(nc, in_maps, core_ids=list(range(ncores)))
    out = np.concatenate([np.asarray(r["y"]).reshape(n_seq, SEQ, D) for r in res.results], axis=0)
    return out.astype(np.float32)
```
